# Optimizing a Trainium2 kernel written in Bass

```python
import math
import jax, jax.numpy as jnp
from jax import lax
import numpy as np

D_MODEL = 4096
BATCH = 4
SEQ = 4096
DEPTH = 1

CHUNK = 64
Q_BLOCK = 128
HEAD_DIM = 128
N_HEADS_DSA = D_MODEL // 2 // HEAD_DIM
N_HEADS_FOX = D_MODEL // 2 // HEAD_DIM
D_LATENT = 512
N_IDX_HEADS = 32
IDX_DIM = 128
TOPK_MAX = 256
D_FF = 11008
N_BUCKETS = 32
MAX_DISTANCE = 128
ALPHA = (2.0 * DEPTH) ** 0.25
BETA = (8.0 * DEPTH) ** -0.25
LN_EPS = 1e-5
RMS_EPS = 1e-6
NEG = -1e30

W_DSA = N_HEADS_DSA * HEAD_DIM
W_FOX = N_HEADS_FOX * HEAD_DIM
MIX_WIDTH = W_DSA + W_FOX
SPLITS = (W_DSA,
          D_LATENT,
          N_IDX_HEADS * IDX_DIM,
          IDX_DIM,
          N_IDX_HEADS,
          W_FOX, W_FOX, W_FOX,
          N_HEADS_FOX)
D_IN = W_DSA + D_LATENT + N_IDX_HEADS * IDX_DIM + IDX_DIM + N_IDX_HEADS + 3 * W_FOX + N_HEADS_FOX

kernel_name = "hybrid_dsa_fox_macaron_deepnorm"


def split_offsets():
    offs, acc = [], 0
    for w in SPLITS[:-1]:
        acc += w
        offs.append(acc)
    return offs


def layer_norm(x, g, b):
    xf = x.astype(jnp.float32)
    mu = jnp.mean(xf, axis=-1, keepdims=True)
    var = jnp.mean(jnp.square(xf - mu), axis=-1, keepdims=True)
    y = (xf - mu) * lax.rsqrt(var + LN_EPS) * g.astype(jnp.float32) + b.astype(jnp.float32)
    return y.astype(x.dtype)


def rms_norm(x, g):
    xf = x.astype(jnp.float32)
    y = xf * lax.rsqrt(jnp.mean(jnp.square(xf), axis=-1, keepdims=True) + RMS_EPS) * g.astype(jnp.float32)
    return y.astype(x.dtype)


def swiglu(x, w_gate, w_up, w_down):
    return (jax.nn.silu(x @ w_gate) * (x @ w_up)) @ w_down


def t5_bucket(rel):
    nb = N_BUCKETS // 2
    max_exact = nb // 2
    ret = jnp.where(rel > 0, nb, 0)
    n = jnp.abs(rel)
    nf = jnp.maximum(n, 1).astype(jnp.float32)
    large = max_exact + (jnp.log(nf / max_exact) / math.log(MAX_DISTANCE / max_exact)
                         * (nb - max_exact)).astype(jnp.int32)
    large = jnp.minimum(large, nb - 1)
    return ret + jnp.where(n < max_exact, n, large)


def to_blocks(a):
    b, s = a.shape[:2]
    return jnp.moveaxis(a.reshape((b, s // Q_BLOCK, Q_BLOCK) + a.shape[2:]), 1, 0)


def from_blocks(a):
    a = jnp.moveaxis(a, 0, 1)
    b, nblk, qb = a.shape[:3]
    return a.reshape(b, nblk * qb, -1)


def dsa_mixer(q, c_kv, q_idx, k_idx, w_idx, w_uk, w_uv, rel_bias):
    b, s = q.shape[:2]
    k_sel = min(TOPK_MAX, s // 4)
    key_chunk = jnp.arange(s) // CHUNK
    scale = HEAD_DIM ** -0.5
    idx_scale = IDX_DIM ** -0.5
    w_scale = N_IDX_HEADS ** -0.5

    def block(args):
        qb, qi, wi, start = args
        t = start + jnp.arange(Q_BLOCK)
        t_chunk = t // CHUNK
        s_h = jnp.einsum('bthd,bsd->bths', qi, k_idx).astype(jnp.float32) * idx_scale
        score = jnp.einsum('bth,bths->bts', wi.astype(jnp.float32) * w_scale, jax.nn.relu(s_h))
        admissible = key_chunk[None, :] <= t_chunk[:, None]
        score = jnp.where(admissible[None], score, -jnp.inf)
        _, idx = lax.top_k(score, k_sel)
        c_sel = jax.vmap(lambda c, i: c[i])(c_kv, idx)
        valid = (idx // CHUNK) <= t_chunk[None, :, None]
        bias = rel_bias[t5_bucket(idx - t[None, :, None])]
        q_lat = jnp.einsum('bthd,hdc->bthc', qb, w_uk)
        logits = (jnp.einsum('bthc,btkc->bthk', q_lat, c_sel).astype(jnp.float32) * scale
                  + jnp.moveaxis(bias, -1, 2).astype(jnp.float32))
        logits = jnp.where(valid[:, :, None, :], logits, NEG)
        p = jax.nn.softmax(logits, axis=-1).astype(c_sel.dtype)
        o_lat = jnp.einsum('bthk,btkc->bthc', p, c_sel)
        return jnp.einsum('bthc,hcd->bthd', o_lat, w_uv)

    starts = jnp.arange(s // Q_BLOCK) * Q_BLOCK
    out = lax.map(block, (to_blocks(q), to_blocks(q_idx), to_blocks(w_idx), starts))
    return from_blocks(out)


def fox_mixer(q, k, v, log_f):
    s = q.shape[1]
    scale = HEAD_DIM ** -0.5
    cum = jnp.cumsum(log_f, axis=1)
    cum_k = jnp.moveaxis(cum, -1, 1)
    key_pos = jnp.arange(s)

    def block(args):
        qb, cq, start = args
        t = start + jnp.arange(Q_BLOCK)
        logits = jnp.einsum('bthd,bshd->bhts', qb, k).astype(jnp.float32) * scale
        logits = logits + (jnp.moveaxis(cq, -1, 1)[..., None] - cum_k[:, :, None, :])
        mask = key_pos[None, :] <= t[:, None]
        logits = jnp.where(mask[None, None], logits, NEG)
        p = jax.nn.softmax(logits, axis=-1).astype(v.dtype)
        return jnp.einsum('bhts,bshd->bthd', p, v)

    starts = jnp.arange(s // Q_BLOCK) * Q_BLOCK
    out = lax.map(block, (to_blocks(q), to_blocks(cum), starts))
    return from_blocks(out)


def hybrid_mixer(h, w_in, b_f, kv_norm_g, idx_k_g, idx_k_b, w_uk, w_uv, rel_bias, w_out):
    b, s, _ = h.shape
    proj = h @ w_in
    q_a, c_kv, q_i, k_i, w_i, q_b, k_b, v_b, f_b = jnp.split(proj, split_offsets(), axis=-1)
    c_kv = rms_norm(c_kv, kv_norm_g)
    k_i = layer_norm(k_i, idx_k_g, idx_k_b)
    o_a = dsa_mixer(q_a.reshape(b, s, N_HEADS_DSA, HEAD_DIM), c_kv,
                    q_i.reshape(b, s, N_IDX_HEADS, IDX_DIM), k_i, w_i, w_uk, w_uv, rel_bias)
    log_f = jax.nn.log_sigmoid(f_b.astype(jnp.float32) + b_f.astype(jnp.float32))
    o_b = fox_mixer(q_b.reshape(b, s, N_HEADS_FOX, HEAD_DIM),
                    k_b.reshape(b, s, N_HEADS_FOX, HEAD_DIM),
                    v_b.reshape(b, s, N_HEADS_FOX, HEAD_DIM), log_f)
    return jnp.concatenate([o_a, o_b], axis=-1) @ w_out


def setup_inputs(seed: int = 0) -> dict:
    key = jax.random.key(seed)
    ks = jax.random.split(key, 24)
    f32 = jnp.float32

    def nrm(k, shape, scale):
        return jax.random.normal(k, shape, f32) * scale

    def gain(k, shape):
        return 1.0 + 0.02 * jax.random.normal(k, shape, f32)

    L = DEPTH
    b_f = (jnp.linspace(1.0, 6.0, N_HEADS_FOX, dtype=f32)[None, :]
           + 0.1 * jax.random.normal(ks[8], (L, N_HEADS_FOX), f32))
    return {
        "x": jax.random.normal(ks[0], (BATCH, SEQ, D_MODEL), f32),
        "ffn1_w_gate": nrm(ks[1], (L, D_MODEL, D_FF), D_MODEL ** -0.5),
        "ffn1_w_up": nrm(ks[2], (L, D_MODEL, D_FF), D_MODEL ** -0.5),
        "ffn1_w_down": nrm(ks[3], (L, D_FF, D_MODEL), BETA * D_FF ** -0.5),
        "ln1_g": gain(ks[4], (L, D_MODEL)),
        "ln1_b": nrm(ks[5], (L, D_MODEL), 0.02),
        "w_in": nrm(ks[6], (L, D_MODEL, D_IN), D_MODEL ** -0.5),
        "b_f": b_f,
        "kv_norm_g": gain(ks[9], (L, D_LATENT)),
        "idx_k_g": gain(ks[10], (L, IDX_DIM)),
        "idx_k_b": nrm(ks[11], (L, IDX_DIM), 0.02),
        "w_uk": nrm(ks[12], (L, N_HEADS_DSA, HEAD_DIM, D_LATENT), D_LATENT ** -0.5),
        "w_uv": nrm(ks[13], (L, N_HEADS_DSA, D_LATENT, HEAD_DIM), D_LATENT ** -0.5),
        "rel_bias": nrm(ks[14], (N_BUCKETS, N_HEADS_DSA), 0.5),
        "w_out": nrm(ks[15], (L, MIX_WIDTH, D_MODEL), BETA * MIX_WIDTH ** -0.5),
        "ln2_g": gain(ks[16], (L, D_MODEL)),
        "ln2_b": nrm(ks[17], (L, D_MODEL), 0.02),
        "ffn2_w_gate": nrm(ks[18], (L, D_MODEL, D_FF), D_MODEL ** -0.5),
        "ffn2_w_up": nrm(ks[19], (L, D_MODEL, D_FF), D_MODEL ** -0.5),
        "ffn2_w_down": nrm(ks[20], (L, D_FF, D_MODEL), BETA * D_FF ** -0.5),
        "ln3_g": gain(ks[21], (L, D_MODEL)),
        "ln3_b": nrm(ks[22], (L, D_MODEL), 0.02),
    }


def reference(x, ffn1_w_gate, ffn1_w_up, ffn1_w_down, ln1_g, ln1_b, w_in, b_f, kv_norm_g,
              idx_k_g, idx_k_b, w_uk, w_uv, rel_bias, w_out, ln2_g, ln2_b,
              ffn2_w_gate, ffn2_w_up, ffn2_w_down, ln3_g, ln3_b):
    h = x
    for l in range(DEPTH):
        h = layer_norm(ALPHA * h + 0.5 * swiglu(h, ffn1_w_gate[l], ffn1_w_up[l], ffn1_w_down[l]),
                       ln1_g[l], ln1_b[l])
        h = layer_norm(ALPHA * h + hybrid_mixer(h, w_in[l], b_f[l], kv_norm_g[l], idx_k_g[l],
                                                idx_k_b[l], w_uk[l], w_uv[l], rel_bias, w_out[l]),
                       ln2_g[l], ln2_b[l])
        h = layer_norm(ALPHA * h + 0.5 * swiglu(h, ffn2_w_gate[l], ffn2_w_up[l], ffn2_w_down[l]),
                       ln3_g[l], ln3_b[l])
    return h
```

```python
import math
import numpy as np
from contextlib import ExitStack
import concourse.bass as bass
import concourse.mybir as mybir
from concourse.bass_utils import run_bass_kernel_spmd

F32 = mybir.dt.float32
BF16 = mybir.dt.bfloat16
AF = mybir.ActivationFunctionType
ALU = mybir.AluOpType
AX = mybir.AxisListType

FULL_CFG = dict(D=4096, FF=11008, S=4096, CL=512, NI=32, TOPK=256, TT=512)
ALPHA = 2.0 ** 0.25
LN_EPS = 1e-5
RMS_EPS = 1e-6
NBK = 32
NEGBIG = -1.0e30
STORE_Q = "pool"


def derive(cfg):
    c = dict(cfg)
    D = c["D"]
    c["KC"] = D // 128
    c["NFC"] = c["FF"] // 128
    c["WA"] = D // 2
    c["WB"] = D // 2
    c["NHA"] = c["WA"] // 128
    c["NHB"] = c["WB"] // 128
    c["NT"] = c["S"] // 128
    c["CC"] = c["CL"] // 128
    NI = c["NI"]
    off = {}
    o = 0
    for nm, w in [("qa", c["WA"]), ("ckv", c["CL"]), ("qi", NI * 128), ("ki", 128), ("wi", NI),
                  ("qb", c["WB"]), ("kb", c["WB"]), ("vb", c["WB"]), ("fb", c["NHB"])]:
        off[nm] = (o, w)
        o += w
    c["off"] = off
    c["DIN"] = o
    fm = []
    for nm in ("qa", "qi", "qb", "kb"):
        s, w = off[nm]
        for i in range(w // 128):
            fm.append((nm, i, s + i * 128))
    c["fm"] = fm
    tm = [[("ckv", off["ckv"][0], c["CL"], 0)],
          [("ki", off["ki"][0], 128, 0), ("wi", off["wi"][0], NI, 128), ("fb", off["fb"][0], c["NHB"], 128 + NI)]]
    vb0, vbw = off["vb"]
    o = 0
    while o < vbw:
        w = min(512, vbw - o)
        tm.append([("vb", vb0 + o, w, 0, o)])
        o += w
    c["tm"] = tm
    c["NDN"] = max(1, D // 512)
    c["DNW"] = min(512, D)
    return c


class Buf:
    __slots__ = ("name", "w", "r", "lsem", "ssem")

    def __init__(self, name):
        self.name = name
        self.w = None
        self.r = {}
        self.lsem = None
        self.ssem = None


class Trk:
    ENG = ("pe", "act", "dve", "pool", "sp")

    def __init__(self, nc, es, n_dma_sems=26):
        self.nc = nc
        self.semobj = {}
        self.semval = {}
        for e in self.ENG:
            self.semobj[e] = es.enter_context(nc.semaphore("sem_" + e))
            self.semval[e] = 0
        self.dma_pool = []
        for i in range(n_dma_sems):
            k = "d%d" % i
            self.semobj[k] = es.enter_context(nc.semaphore("sem_" + k))
            self.semval[k] = 0
            self.dma_pool.append(k)
        self.free_dma = list(self.dma_pool)
        self.waited = {e: {} for e in self.ENG}
        self.prog = {e: [] for e in self.ENG}
        self.phase_dma = set()
        self.store_q = STORE_Q

    def new_phase(self):
        self.store_q = STORE_Q
        self.free_dma = list(self.dma_pool)
        self.phase_dma = set()

    def _dsem(self):
        k = self.free_dma.pop(0)
        self.phase_dma.add(k)
        return k

    def _wait(self, eng, deps):
        wd = self.waited[eng]
        best = {}
        for (k, v) in deps:
            if k == eng and eng in ("pe", "sp"):
                continue
            if wd.get(k, 0) >= v:
                continue
            if best.get(k, 0) < v:
                best[k] = v
        for k, v in best.items():
            wd[k] = v
            so = self.semobj[k]
            self.prog[eng].append(lambda h, so=so, v=v: h.wait_ge(so, v))

    def _deps(self, reads, writes):
        deps = []
        for b in reads:
            if b.w is not None:
                deps.append(b.w)
        for b in writes:
            if b.w is not None:
                deps.append(b.w)
            deps.extend(b.r.items())
        return deps

    def op(self, eng, fn, reads=(), writes=()):
        self._wait(eng, self._deps(reads, writes))
        self.semval[eng] += 1
        v = self.semval[eng]
        so = self.semobj[eng]
        self.prog[eng].append(lambda h, fn=fn, so=so: fn(h).then_inc(so, 1))
        for b in writes:
            b.w = (eng, v)
            b.r = {}
        for b in reads:
            if b.r.get(eng, 0) < v:
                b.r[eng] = v

    def dma(self, out_ap, in_ap, reads=(), writes=(), q=None, **kw):
        if q is None:
            q = "sp" if writes else self.store_q
        if writes:
            b = writes[0]
            if b.lsem is None:
                b.lsem = self._dsem()
            k = b.lsem
        else:
            b = reads[0]
            if b.ssem is None:
                b.ssem = self._dsem()
            k = b.ssem
        self._wait(q, [d for d in self._deps(reads, writes) if d[0] != k])
        self.semval[k] += 16
        v = self.semval[k]
        so = self.semobj[k]
        self.prog[q].append(lambda h, so=so, o=out_ap, i=in_ap, kw=kw: h.dma_start(out=o, in_=i, **kw).then_inc(so, 16))
        for b in writes:
            b.w = (k, v)
            b.r = {}
        for b in reads:
            if b.r.get(k, 0) < v:
                b.r[k] = v

    def drain_dma(self):
        deps = [(k, self.semval[k]) for k in self.phase_dma]
        self._wait("sp", deps)

    def emit_block(self):
        nc = self.nc
        progs = self.prog
        self.prog = {e: [] for e in self.ENG}
        with nc.Block() as block:
            @block.sync
            def _(h):
                for t in progs["sp"]:
                    t(h)

            @block.tensor
            def _(h):
                for t in progs["pe"]:
                    t(h)

            @block.scalar
            def _(h):
                for t in progs["act"]:
                    t(h)

            @block.vector
            def _(h):
                for t in progs["dve"]:
                    t(h)

            @block.gpsimd
            def _(h):
                for t in progs["pool"]:
                    t(h)


class Ctx:
    pass


_SBN = [0]


def sb(es, nc, name, shape, dt):
    _SBN[0] += 1
    return es.enter_context(nc.sbuf_tensor("%s_%d" % (name, _SBN[0]), list(shape), dt))


def build_program(cfg, stop_after=None):
    c = derive(cfg)
    D, FF, S, KC, NFC, NT, TT = c["D"], c["FF"], c["S"], c["KC"], c["NFC"], c["NT"], c["TT"]
    CL, CC, NI, NHA, NHB, WA, WB, DIN = c["CL"], c["CC"], c["NI"], c["NHA"], c["NHB"], c["WA"], c["WB"], c["DIN"]
    NDN, DNW = c["NDN"], c["DNW"]
    TOPK = c["TOPK"]
    fm, tm, off = c["fm"], c["tm"], c["off"]
    NFM = len(fm)
    NTM = len(tm)
    SL = 16
    NSL = (NFC + SL - 1) // SL
    TBT = TT // 128
    assert S % TT == 0
    EOPS_PER_SITE = cfg.get("EOPS_PER_SITE", 8)
    BG_IDX = cfg.get("BG_IDX", 4)
    BG_ATT = cfg.get("BG_ATT", 4)
    SO = S // 2
    NTO = NT // 2
    assert SO % TT == 0
    NCST = 13

    nc = bass.Bass("TRN2", target_bir_lowering=False)

    def din(name, shape, dt=F32):
        return nc.dram_tensor(name, list(shape), dt, kind="ExternalInput").ap()

    def dscr(name, shape, dt):
        return nc.dram_tensor(name, list(shape), dt, kind="Internal").ap()

    x = din("x", [S, D])
    w_g = [din("ffn1_w_gate", [D, FF]), din("ffn2_w_gate", [D, FF])]
    w_u = [din("ffn1_w_up", [D, FF]), din("ffn2_w_up", [D, FF])]
    w_d = [din("ffn1_w_down", [FF, D]), din("ffn2_w_down", [FF, D])]
    ln_g = [din("ln1_g", [1, D]), din("ln2_g", [1, D]), din("ln3_g", [1, D])]
    ln_b = [din("ln1_b", [1, D]), din("ln2_b", [1, D]), din("ln3_b", [1, D])]
    w_in = din("w_in", [D, DIN])
    b_f = din("b_f", [1, NHB])
    kv_g = din("kv_norm_g", [1, CL])
    ik_g = din("idx_k_g", [1, 128])
    ik_b = din("idx_k_b", [1, 128])
    w_uk = din("w_uk", [NHA, 128, CL])
    w_uv = din("w_uv", [NHA, CL, 128])
    rel_bias = din("rel_bias", [1, NBK * NHA])
    w_out = din("w_out", [D, D])
    cst = din("consts", [128, NCST, 128])
    out = nc.dram_tensor("out", [SO, D], F32, kind="ExternalOutput").ap()

    gu_s = [dscr("gu1_s", [NFC, 128, 2 * KC * 128], BF16), dscr("gu2_s", [NFC, 128, 2 * KC * 128], BF16)]
    d_s = [dscr("d1_s", [NDN * NSL, 128, SL * DNW], BF16), dscr("d2_s", [NDN * NSL, 128, SL * DNW], BF16)]
    win_fm = dscr("win_fm", [NFM, 128, KC * 128], BF16)
    win_tm = dscr("win_tm", [NTM, 128, KC * 512], BF16)
    wout_s = dscr("wout_s", [NDN, 128, KC * DNW], BF16)
    r_pre = dscr("r_pre", [S, D], F32)
    h1 = dscr("h1", [S, D], F32)
    h2 = dscr("h2", [SO, D], F32)
    qaT = dscr("qaT", [WA, SO], BF16)
    qiT = dscr("qiT", [NI * 128, SO], BF16)
    qbT = dscr("qbT", [WB, SO], BF16)
    kbT = dscr("kbT", [WB, S], BF16)
    ckvT = dscr("ckvT", [CL, S], BF16)
    kiT = dscr("kiT", [128, S], BF16)
    vb_d = dscr("vb_d", [S, WB], BF16)
    wia_d = dscr("wia_d", [S, NI], F32)
    wis_d = dscr("wis_d", [S, NI], F32)
    lf_d = dscr("lf_d", [S, NHB], F32)
    maskT_d = dscr("maskT_d", [NTO * NT, 128, 128], BF16)
    oT = dscr("oT", [D, SO], BF16)
    En_d = dscr("En_d", [128, NHA * 3 * 128], BF16)

    es = ExitStack()
    T = Trk(nc, es)
    psum = es.enter_context(nc.psum_tensor("psum", [128, 8 * 512], F32))
    PB = [Buf("ps%d" % i) for i in range(8)]

    def ps(i, n=512):
        return psum[:, i * 512:i * 512 + n]

    def ps_bf(i):
        return psum[:, i * 512:(i + 1) * 512].bitcast(BF16)

    cast_rr = [0]

    def cast(out_ap, in_ap, rd, wr, engs=("dve", "pool", "act")):
        e = engs[cast_rr[0] % len(engs)]
        cast_rr[0] += 1
        if e == "act":
            T.op("act", lambda h: h.activation(out=out_ap, in_=in_ap, func=AF.Copy), rd, wr)
        else:
            T.op(e, lambda h: h.tensor_copy(out=out_ap, in_=in_ap), rd, wr)

    def end_phase():
        T.drain_dma()
        T.emit_block()
        T.new_phase()

    def ffn_units(f, usz=4096):
        units = []
        for m, W in enumerate((w_g[f], w_u[f])):
            Wv = W.rearrange("(kc p) f -> p kc f", p=128)
            for fc in range(NFC):
                for k0 in range(0, KC, 8):
                    k1 = min(KC, k0 + 8)
                    units.append((Wv[:, k0:k1, fc * 128:(fc + 1) * 128],
                                  gu_s[f][fc, :, (m * KC + k0) * 128:(m * KC + k1) * 128], k1 - k0, 128, False))
        Wv = w_d[f].rearrange("(j p) d -> p j d", p=128)
        jstep = max(1, usz // DNW)
        for n in range(NDN):
            for q in range(NSL):
                j0 = q * SL
                j1 = min(NFC, j0 + SL)
                for ja in range(j0, j1, jstep):
                    jb = min(j1, ja + jstep)
                    units.append((Wv[:, ja:jb, n * DNW:(n + 1) * DNW],
                                  d_s[f][n * NSL + q, :, (ja - j0) * DNW:(jb - j0) * DNW], jb - ja, DNW, False))
        return units

    def win_units(usz=4096):
        units = []
        Wv = w_in.rearrange("(kc p) f -> p kc f", p=128)
        for i, (nm, ci, col) in enumerate(fm):
            for k0 in range(0, KC, 8):
                k1 = min(KC, k0 + 8)
                units.append((Wv[:, k0:k1, col:col + 128], win_fm[i, :, k0 * 128:k1 * 128], k1 - k0, 128, False))
        for i, segs in enumerate(tm):
            for seg in segs:
                col, w, dofs = seg[1], seg[2], seg[3]
                kstep = max(1, usz // w)
                dv = win_tm[i].rearrange("p (kc f) -> p kc f", f=512)
                for k0 in range(0, KC, kstep):
                    k1 = min(KC, k0 + kstep)
                    units.append((Wv[:, k0:k1, col:col + w], dv[:, k0:k1, dofs:dofs + w], k1 - k0, w, True))
        return units

    def wout_units(usz=4096):
        units = []
        Wv = w_out.rearrange("(kc p) f -> p kc f", p=128)
        kstep = max(1, usz // DNW)
        for n in range(NDN):
            for k0 in range(0, KC, kstep):
                k1 = min(KC, k0 + kstep)
                units.append((Wv[:, k0:k1, n * DNW:(n + 1) * DNW], wout_s[n, :, k0 * DNW:k1 * DNW], k1 - k0, DNW, False))
        return units

    class Repacker:
        def __init__(self, units, engs, lq=None, sq=None, nb=4, la=2, usz=4096):
            self.usz = usz
            self.units = units
            self.engs = engs
            self.lq, self.sq = lq, sq
            self.nb, self.la = nb, la
            self.nl = 0
            self.ncst = 0
            self.rr = 0
            self.att = False

        def attach(self, pes, tag):
            self.sin = [sb(pes, nc, "%s_in%d" % (tag, i), [128, self.usz], F32) for i in range(self.nb)]
            self.sout = [sb(pes, nc, "%s_out%d" % (tag, i), [128, self.usz], BF16) for i in range(self.nb)]
            self.bin = [Buf("rpi%d" % i) for i in range(self.nb)]
            self.bout = [Buf("rpo%d" % i) for i in range(self.nb)]
            self.att = True

        def _load(self):
            src_ap, dst_ap, a, b, dst3 = self.units[self.nl]
            bi = self.nl % self.nb
            T.dma(self.sin[bi][:, :a * b].rearrange("p (a b) -> p a b", a=a), src_ap, writes=[self.bin[bi]], q=self.lq)
            self.nl += 1

        def _finish(self):
            src_ap, dst_ap, a, b, dst3 = self.units[self.ncst]
            bi = self.ncst % self.nb
            n = a * b
            e = self.engs[self.rr % len(self.engs)]
            self.rr += 1
            o_ap, i_ap = self.sout[bi][:, :n], self.sin[bi][:, :n]
            if e == "act":
                T.op("act", lambda h: h.activation(out=o_ap, in_=i_ap, func=AF.Copy), [self.bin[bi]], [self.bout[bi]])
            else:
                T.op(e, lambda h: h.tensor_copy(out=o_ap, in_=i_ap), [self.bin[bi]], [self.bout[bi]])
            src = self.sout[bi][:, :n].rearrange("p (a b) -> p a b", a=a) if dst3 else self.sout[bi][:, :n]
            T.dma(dst_ap, src, reads=[self.bout[bi]], q=self.sq)
            self.ncst += 1

        def step(self, k=1):
            if not self.att:
                return
            for _ in range(k):
                if self.nl < len(self.units):
                    self._load()
                if self.ncst < self.nl and (self.nl - self.ncst > self.la or self.nl == len(self.units)):
                    self._finish()

        def flush_inflight(self):
            while self.ncst < self.nl:
                self._finish()
            self.att = False

        def run_all(self):
            while self.ncst < len(self.units):
                self.step(1)

        def done(self):
            return self.ncst == len(self.units)

    def phase_precast():
        with ExitStack() as pes:
            rp = Repacker(ffn_units(0), ["dve", "act", "dve", "act", "dve", "act", "pool"])
            rp.attach(pes, "p0")
            rp.run_all()
            rp.flush_inflight()
            end_phase()

    BGU = 1024
    bg = Repacker(win_units(BGU) + ffn_units(1, BGU) + wout_units(BGU), ["dve", "act"], lq="pool", sq="pool", nb=3, la=2, usz=BGU)

    def load_transpose(pes_bufs, src, t0, ntok, dstT, dstT_bufs, ident_f, ident_buf, stage, stage_bufs, ps_ids):
        for it in lt_items(src, t0, ntok, dstT, dstT_bufs, ident_f, ident_buf, stage, stage_bufs, ps_ids):
            it()

    def lt_items(src, t0, ntok, dstT, dstT_bufs, ident_f, ident_buf, stage, stage_bufs, ps_ids, dq=None):
        SW = stage[0].shape[1]
        KW = SW // 128
        pieces = [(tb, c0) for tb in range(ntok // 128) for c0 in range(0, D, SW)]
        ns = len(stage)

        def load(i):
            tb, c0 = pieces[i]
            si = i % ns
            T.dma(stage[si][:, :SW], src[t0 + tb * 128:t0 + (tb + 1) * 128, c0:c0 + SW], writes=[stage_bufs[si]], q=dq)

        def piece(i):
            tb, c0 = pieces[i]
            si = i % ns
            kbase = c0 // 128
            for k0 in range(0, KW, 4):
                k1 = min(KW, k0 + 4)
                pi = ps_ids[(k0 // 4) % len(ps_ids)]
                for kc in range(k0, k1):
                    T.op("pe", lambda h, kc=kc, pi=pi, k0=k0, si=si: h.transpose(
                        out=ps(pi)[:, (kc - k0) * 128:(kc - k0 + 1) * 128],
                        in_=stage[si][:, kc * 128:(kc + 1) * 128], identity=ident_f),
                        [stage_bufs[si], ident_buf], [PB[pi]])
                n = (k1 - k0) * 128
                e = "act" if (k0 // 4) % 2 == 0 else "dve"
                o_ap = dstT[:, kbase + k0:kbase + k1, tb * 128:(tb + 1) * 128]
                i_ap = ps(pi)[:, :n].rearrange("p (a b) -> p a b", a=k1 - k0)
                if e == "act":
                    T.op("act", lambda h, o_ap=o_ap, i_ap=i_ap: h.activation(out=o_ap, in_=i_ap, func=AF.Copy),
                         [PB[pi]], [dstT_bufs[tb]])
                else:
                    T.op("dve", lambda h, o_ap=o_ap, i_ap=i_ap: h.tensor_copy(out=o_ap, in_=i_ap),
                         [PB[pi]], [dstT_bufs[tb]])
            if i + ns < len(pieces):
                load(i + ns)

        items = [lambda: [load(i) for i in range(min(ns, len(pieces)))]]
        items += [lambda i=i: piece(i) for i in range(len(pieces))]
        return items

    def load_consts(pes):
        cs = sb(pes, nc, "cst", [128, NCST, 128], F32)
        cb = Buf("cst")
        T.dma(cs[:], cst, writes=[cb])
        return cs, cb

    def phase_ffn(f, src, ntok):
        with ExitStack() as pes:
            idt = sb(pes, nc, "idt", [128, 128], F32)
            cb = Buf("idt")
            T.dma(idt[:], cst[:, 0, :], writes=[cb])
            ident_f = idt[:]
            if f == 0:
                bg.attach(pes, "bgf")
            xT = sb(pes, nc, "xT", [128, KC, TT], BF16)
            xT_b = [Buf("xT%d" % i) for i in range(TBT)]
            aT = sb(pes, nc, "aT", [128, NFC, TT], BF16)
            aT_b = [Buf("aT%d" % i) for i in range(NFC)]
            NWB = 3
            WSZ = max(2 * KC * 128, SL * DNW)
            wb_t = [sb(pes, nc, "wb%d" % i, [128, WSZ], BF16) for i in range(NWB)]
            wb_b = [Buf("wb%d" % i) for i in range(NWB)]
            stage = [sb(pes, nc, "stg%d" % i, [128, D // 2], F32) for i in range(1)]
            stage_b = [Buf("stg%d" % i) for i in range(1)]
            sil = [sb(pes, nc, "sil%d" % i, [128, TT], BF16) for i in range(2)]
            sil_b = [Buf("sil%d" % i) for i in range(2)]
            xp = [sb(pes, nc, "xp%d" % i, [128, DNW], F32) for i in range(2)]
            xp_b = [Buf("xp%d" % i) for i in range(2)]
            rp = [sb(pes, nc, "rp%d" % i, [128, DNW], F32) for i in range(2)]
            rp_b = [Buf("rp%d" % i) for i in range(2)]
            wcnt = [0]
            pcnt = [0]
            nxt_items = lt_items(src, 0, TT, xT, xT_b, ident_f, cb, stage, stage_b, [2, 3], dq="act")
            for t0 in range(0, ntok, TT):
                for it in nxt_items:
                    it()
                nxt_items = lt_items(src, t0 + TT, TT, xT, xT_b, ident_f, cb, stage, stage_b, [2, 3], dq="act") if t0 + TT < ntok else []
                for fc in range(NFC):
                    if f == 0:
                        bg.step(3 if fc % 2 == 0 else 2)
                    wi = wcnt[0] % NWB
                    wcnt[0] += 1
                    T.dma(wb_t[wi][:, :2 * KC * 128], gu_s[f][fc], writes=[wb_b[wi]])
                    wv = wb_t[wi][:, :2 * KC * 128].rearrange("p (m kc f) -> p m kc f", m=2, kc=KC)
                    pg = 2 + (fc % 2) * 2
                    pu = pg + 1
                    for m, pi in ((0, pg), (1, pu)):
                        for kc in range(KC):
                            T.op("pe", lambda h, m=m, pi=pi, kc=kc, wv=wv: h.matmul(
                                out=ps(pi)[:, :TT], lhsT=wv[:, m, kc, :], rhs=xT[:, kc, :],
                                start=(kc == 0), stop=(kc == KC - 1)),
                                [wb_b[wi]] + xT_b, [PB[pi]])
                    si = fc % 2
                    T.op("act", lambda h, si=si, pg=pg: h.activation(out=sil[si][:], in_=ps(pg)[:, :TT], func=AF.Silu),
                         [PB[pg]], [sil_b[si]])
                    T.op("dve", lambda h, si=si, pu=pu, fc=fc: h.tensor_tensor(
                        out=aT[:, fc, :], in0=ps(pu)[:, :TT], in1=sil[si][:], op=ALU.mult),
                        [PB[pu], sil_b[si]], [aT_b[fc]])
                for n in range(NDN):
                    for q in range(NSL):
                        j0 = q * SL
                        j1 = min(NFC, j0 + SL)
                        wi = wcnt[0] % NWB
                        wcnt[0] += 1
                        T.dma(wb_t[wi][:, :(j1 - j0) * DNW], d_s[f][n * NSL + q, :, :(j1 - j0) * DNW], writes=[wb_b[wi]])
                        wv = wb_t[wi][:, :SL * DNW].rearrange("p (j d) -> p j d", j=SL)
                        for tb in range(TBT):
                            pi = tb if TBT <= 2 else (tb if tb < 2 else 4 + tb)
                            for j in range(j0, j1):
                                T.op("pe", lambda h, pi=pi, j=j, j0=j0, tb=tb, wv=wv: h.matmul(
                                    out=ps(pi)[:, :DNW], lhsT=aT[:, j, tb * 128:(tb + 1) * 128], rhs=wv[:, j - j0, :],
                                    start=(j == 0), stop=(j == NFC - 1)),
                                    [wb_b[wi], aT_b[j]], [PB[pi]])
                        if nxt_items and n >= 1:
                            nxt_items.pop(0)()
                    for tb in range(TBT):
                        pi = tb if TBT <= 2 else (tb if tb < 2 else 4 + tb)
                        k = pcnt[0] % 2
                        pcnt[0] += 1
                        rows = slice(t0 + tb * 128, t0 + (tb + 1) * 128)
                        T.dma(xp[k][:], src[rows, n * DNW:(n + 1) * DNW], writes=[xp_b[k]])
                        T.op("act", lambda h, k=k, pi=pi: h.activation(out=rp[k][:], in_=ps(pi)[:, :DNW], func=AF.Copy, scale=0.5),
                             [PB[pi]], [rp_b[k]])
                        T.op("dve", lambda h, k=k: h.scalar_tensor_tensor(
                            out=rp[k][:], in0=xp[k][:], scalar=float(ALPHA), in1=rp[k][:], op0=ALU.mult, op1=ALU.add),
                            [xp_b[k], rp_b[k]], [rp_b[k]])
                        T.dma(r_pre[rows, n * DNW:(n + 1) * DNW], rp[k][:], reads=[rp_b[k]])
            if f == 0:
                bg.flush_inflight()
            end_phase()

    def phase_ln(li, dst, ntok):
        if li == 0 and not bg.done():
            with ExitStack() as pes:
                bg.engs = ["dve", "act"]
                bg.lq, bg.sq = None, None
                bg.attach(pes, "bgl")
                bg.run_all()
                bg.flush_inflight()
                end_phase()
        with ExitStack() as pes:
            gt = sb(pes, nc, "ln_g", [128, D], F32)
            bt = sb(pes, nc, "ln_b", [128, D], F32)
            gb, bb = Buf("g"), Buf("b")
            T.dma(gt[:], ln_g[li].partition_broadcast(128), writes=[gb])
            T.dma(bt[:], ln_b[li].partition_broadcast(128), writes=[bb])
            NR = 5
            rows = [sb(pes, nc, "lnr%d" % i, [128, D], F32) for i in range(NR)]
            rows_b = [Buf("lnr%d" % i) for i in range(NR)]
            nst = (D + 511) // 512
            st = [sb(pes, nc, "lnst%d" % i, [128, nst * 6], F32) for i in range(NR)]
            mv = [sb(pes, nc, "lnmv%d" % i, [128, 4], F32) for i in range(NR)]
            sm_b = [Buf("lnsm%d" % i) for i in range(NR)]
            for tb in range(ntok // 128):
                i = tb % NR
                T.dma(rows[i][:], r_pre[tb * 128:(tb + 1) * 128, :], writes=[rows_b[i]])
                for s_ in range(nst):
                    w = min(512, D - s_ * 512)
                    T.op("dve", lambda h, i=i, s_=s_, w=w: h.bn_stats(out=st[i][:, s_ * 6:(s_ + 1) * 6], in_=rows[i][:, s_ * 512:s_ * 512 + w]),
                         [rows_b[i]], [sm_b[i]])
                T.op("dve", lambda h, i=i: h.bn_aggr(out=mv[i][:, 0:2], in_=st[i][:]), [sm_b[i]], [sm_b[i]])
                T.op("dve", lambda h, i=i: h.tensor_scalar(out=mv[i][:, 2:3], in0=mv[i][:, 1:2], scalar1=float(LN_EPS), scalar2=None, op0=ALU.add),
                     [sm_b[i]], [sm_b[i]])
                T.op("act", lambda h, i=i: h.activation(out=mv[i][:, 2:3], in_=mv[i][:, 2:3], func=AF.Sqrt), [sm_b[i]], [sm_b[i]])
                T.op("dve", lambda h, i=i: h.reciprocal(out=mv[i][:, 3:4], in_=mv[i][:, 2:3]), [sm_b[i]], [sm_b[i]])
                T.op("dve", lambda h, i=i: h.tensor_scalar(out=mv[i][:, 2:3], in0=mv[i][:, 0:1], scalar1=mv[i][:, 3:4], scalar2=-1.0,
                                                           op0=ALU.mult, op1=ALU.mult), [sm_b[i]], [sm_b[i]])
                T.op("act", lambda h, i=i: h.activation(out=rows[i][:], in_=rows[i][:], func=AF.Identity, scale=mv[i][:, 3:4], bias=mv[i][:, 2:3]),
                     [sm_b[i], rows_b[i]], [rows_b[i]])
                T.op("pool", lambda h, i=i: h.tensor_tensor(out=rows[i][:], in0=rows[i][:], in1=gt[:], op=ALU.mult), [rows_b[i], gb], [rows_b[i]])
                T.op("dve", lambda h, i=i: h.tensor_tensor(out=rows[i][:], in0=rows[i][:], in1=bt[:], op=ALU.add), [rows_b[i], bb], [rows_b[i]])
                T.dma(dst[tb * 128:(tb + 1) * 128, :], rows[i][:], reads=[rows_b[i]])
            end_phase()

    def phase_win():
        with ExitStack() as pes:
            cs, cb = load_consts(pes)
            ident_f = cs[:, 0, :]
            hT = sb(pes, nc, "hT", [128, KC, TT], BF16)
            hT_b = [Buf("hT%d" % i) for i in range(TBT)]
            stage = [sb(pes, nc, "stg%d" % i, [128, D], F32) for i in range(1)]
            stage_b = [Buf("stg%d" % i) for i in range(1)]
            NWB = 2
            wb_t = [sb(pes, nc, "wb%d" % i, [128, KC * 512], BF16) for i in range(NWB)]
            wb_b = [Buf("wb%d" % i) for i in range(NWB)]
            NFB = 4
            fb_t = [sb(pes, nc, "fmb%d" % i, [128, KC * 128], BF16) for i in range(NFB)]
            fb_b = [Buf("fmb%d" % i) for i in range(NFB)]
            fcnt = [0]
            ev = [sb(pes, nc, "ev%d" % i, [128, 512], BF16) for i in range(2)]
            ev_b = [Buf("ev%d" % i) for i in range(2)]
            evf = [sb(pes, nc, "evf%d" % i, [128, 512], F32) for i in range(2)]
            evf_b = [Buf("evf%d" % i) for i in range(2)]
            sq = sb(pes, nc, "sq", [128, 512], F32)
            sq_b = Buf("sq")
            sm = [sb(pes, nc, "wsm%d" % i, [128, 16], F32) for i in range(2)]
            sm_b = [Buf("wsm%d" % i) for i in range(2)]
            tr = [sb(pes, nc, "trn%d" % i, [128, 512], BF16) for i in range(2)]
            tr_b = [Buf("trn%d" % i) for i in range(2)]
            identb = sb(pes, nc, "identb", [128, 128], BF16)
            identb_b = Buf("identb")
            T.op("dve", lambda h: h.tensor_copy(out=identb[:], in_=ident_f), [cb], [identb_b])
            kvg = sb(pes, nc, "kvg", [128, CL], F32)
            ikg = sb(pes, nc, "ikg", [128, 128], F32)
            ikb = sb(pes, nc, "ikb", [128, 128], F32)
            bft = sb(pes, nc, "bft", [128, NHB], F32)
            par_b = Buf("par")
            T.dma(kvg[:], kv_g.partition_broadcast(128), writes=[par_b])
            pb2, pb3, pb4 = Buf("p2"), Buf("p3"), Buf("p4")
            T.dma(ikg[:], ik_g.partition_broadcast(128), writes=[pb2])
            T.dma(ikb[:], ik_b.partition_broadcast(128), writes=[pb3])
            T.dma(bft[:], b_f.partition_broadcast(128), writes=[pb4])
            eops = []

            def defer(*a_, **k_):
                eops.append(("op", a_, k_))

            def defer_dma(*a_, **k_):
                eops.append(("dma", a_, k_))

            def drain_eops(n):
                for _ in range(min(n, len(eops))):
                    kind_, a_, k_ = eops.pop(0)
                    if kind_ == "op":
                        T.op(*a_, **k_)
                    else:
                        T.dma(*a_, **k_)

            rbb = sb(pes, nc, "rbb", [128, NBK * NHA], F32)
            rbb_b = Buf("rbb")
            T.dma(rbb[:], rel_bias.partition_broadcast(128), writes=[rbb_b])
            Ef = sb(pes, nc, "Ef", [128, 3, 128], F32)
            Ef_b = Buf("Ef")
            oh = sb(pes, nc, "oh", [128, 3, 128], F32)
            oh_b = Buf("oh")
            En = sb(pes, nc, "En", [128, NHA, 3, 128], BF16)
            En_b = Buf("En")
            negc = sb(pes, nc, "negc", [128, NHA], F32)
            negc_b = Buf("negc")
            for hh in range(NHA):
                defer("dve", lambda h, hh=hh: h.tensor_scalar(out=negc[:, hh:hh + 1], in0=rbb[:, 15 * NHA + hh:15 * NHA + hh + 1], scalar1=-1.0, scalar2=None,
                                                             op0=ALU.mult), [rbb_b], [negc_b])
            for hh in range(NHA):
                for b in range(NBK):
                    for j, bm in enumerate((cs[:, 4, :], cs[:, 5, :], cs[:, 6, :])):
                        defer("dve", lambda h, b=b, j=j, bm=bm: h.tensor_scalar(out=oh[:, j, :], in0=bm, scalar1=float(b), scalar2=None, op0=ALU.is_equal),
                             [cb], [oh_b])
                        if b == 0:
                            defer("dve", lambda h, b=b, j=j, hh=hh: h.tensor_scalar(out=Ef[:, j, :], in0=oh[:, j, :], scalar1=rbb[:, b * NHA + hh:b * NHA + hh + 1],
                                                                                   scalar2=None, op0=ALU.mult), [oh_b, rbb_b], [Ef_b])
                        else:
                            defer("dve", lambda h, b=b, j=j, hh=hh: h.scalar_tensor_tensor(out=Ef[:, j, :], in0=oh[:, j, :], scalar=rbb[:, b * NHA + hh:b * NHA + hh + 1],
                                                                                          in1=Ef[:, j, :], op0=ALU.mult, op1=ALU.add), [oh_b, rbb_b, Ef_b], [Ef_b])
                defer("act", lambda h, hh=hh: h.activation(out=Ef[:, :, :], in_=Ef[:, :, :], func=AF.Exp, bias=negc[:, hh:hh + 1]), [Ef_b, negc_b], [Ef_b])
                defer("dve", lambda h, hh=hh: h.tensor_tensor(out=En[:, hh, 0, :], in0=Ef[:, 0, :], in1=cs[:, 3, :], op=ALU.mult), [Ef_b, cb], [En_b])
                defer("dve", lambda h, hh=hh: h.tensor_copy(out=En[:, hh, 1:3, :], in_=Ef[:, 1:3, :]), [Ef_b], [En_b])

            defer_dma(En_d, En[:].rearrange("p a b c -> p (a b c)"), reads=[En_b])
            wcnt = [0]
            ecnt = [0]
            dstmap = {"qa": qaT, "qi": qiT, "qb": qbT, "kb": kbT}
            for t0 in range(0, S, TT):
                load_transpose(None, h1, t0, TT, hT, hT_b, ident_f, cb, stage, stage_b, [0, 1])
                for i, (nm, ci, col) in enumerate(fm):
                    if t0 >= SO and nm != "kb":
                        continue
                    wi = fcnt[0] % NFB
                    fcnt[0] += 1
                    T.dma(fb_t[wi][:], win_fm[i], writes=[fb_b[wi]])
                    wv = fb_t[wi][:].rearrange("p (kc f) -> p kc f", kc=KC)
                    pi = 2 + (i % 2)
                    for kc in range(KC):
                        T.op("pe", lambda h, pi=pi, kc=kc, wv=wv: h.matmul(
                            out=ps(pi)[:, :TT], lhsT=wv[:, kc, :], rhs=hT[:, kc, :], start=(kc == 0), stop=(kc == KC - 1)),
                            [fb_b[wi]] + hT_b, [PB[pi]])
                    k = ecnt[0] % 2
                    ecnt[0] += 1
                    if k == 0:
                        T.op("act", lambda h, k=k, pi=pi: h.activation(out=ev[k][:, :TT], in_=ps(pi)[:, :TT], func=AF.Copy), [PB[pi]], [ev_b[k]])
                    else:
                        T.op("dve", lambda h, k=k, pi=pi: h.tensor_copy(out=ev[k][:, :TT], in_=ps(pi)[:, :TT]), [PB[pi]], [ev_b[k]])
                    T.dma(dstmap[nm][ci * 128:(ci + 1) * 128, t0:t0 + TT], ev[k][:, :TT], reads=[ev_b[k]])
                    drain_eops(EOPS_PER_SITE)
                for i, segs in enumerate(tm):
                    wi = wcnt[0] % NWB
                    wcnt[0] += 1
                    T.dma(wb_t[wi][:], win_tm[i], writes=[wb_b[wi]])
                    wv = wb_t[wi][:].rearrange("p (kc f) -> p kc f", kc=KC)
                    wtot = max(s_[3] + s_[2] for s_ in segs)
                    for tb in range(TBT):
                        rows = slice(t0 + tb * 128, t0 + (tb + 1) * 128)
                        pi = 4 + (tb % 2)
                        for kc in range(KC):
                            T.op("pe", lambda h, pi=pi, kc=kc, wv=wv, tb=tb, wtot=wtot: h.matmul(
                                out=ps(pi)[:, :wtot], lhsT=hT[:, kc, tb * 128:(tb + 1) * 128], rhs=wv[:, kc, :wtot],
                                start=(kc == 0), stop=(kc == KC - 1)), [wb_b[wi], hT_b[tb]], [PB[pi]])
                        k = ecnt[0] % 2
                        ecnt[0] += 1
                        nm = segs[0][0]
                        if nm == "vb":
                            w, vo = segs[0][2], segs[0][4]
                            T.op("act", lambda h, k=k, pi=pi, w=w: h.activation(out=ev[k][:, :w], in_=ps(pi)[:, :w], func=AF.Copy), [PB[pi]], [ev_b[k]])
                            T.dma(vb_d[rows, vo:vo + w], ev[k][:, :w], reads=[ev_b[k]])
                        elif nm == "ckv":
                            T.op("act", lambda h, k=k, pi=pi: h.activation(out=sq[:, :CL], in_=ps(pi)[:, :CL], func=AF.Square, accum_out=sm[k][:, 0:1]),
                                 [PB[pi]], [sq_b, sm_b[k]])
                            T.op("dve", lambda h, k=k: h.tensor_scalar(out=sm[k][:, 1:2], in0=sm[k][:, 0:1], scalar1=1.0 / CL, scalar2=float(RMS_EPS),
                                                                       op0=ALU.mult, op1=ALU.add), [sm_b[k]], [sm_b[k]])
                            T.op("act", lambda h, k=k: h.activation(out=sm[k][:, 1:2], in_=sm[k][:, 1:2], func=AF.Sqrt), [sm_b[k]], [sm_b[k]])
                            T.op("dve", lambda h, k=k: h.reciprocal(out=sm[k][:, 2:3], in_=sm[k][:, 1:2]), [sm_b[k]], [sm_b[k]])
                            T.op("dve", lambda h, k=k, pi=pi: h.scalar_tensor_tensor(out=tr[k][:, :CL], in0=ps(pi)[:, :CL], scalar=sm[k][:, 2:3], in1=kvg[:],
                                                                                 op0=ALU.mult, op1=ALU.mult), [PB[pi], sm_b[k], par_b], [tr_b[k]])
                            pt = 6 + (tb % 2)
                            for cc in range(CC):
                                T.op("pe", lambda h, k=k, cc=cc, pt=pt: h.transpose(out=ps_bf(pt)[:, cc * 128:(cc + 1) * 128],
                                                                                  in_=tr[k][:, cc * 128:(cc + 1) * 128], identity=identb[:]),
                                     [tr_b[k], identb_b], [PB[pt]])
                            T.op("act", lambda h, k=k, pt=pt: h.activation(out=ev[k][:, :CL], in_=ps_bf(pt)[:, :CL], func=AF.Copy), [PB[pt]], [ev_b[k]])
                            T.dma(ckvT.rearrange("(cc p) s -> p cc s", p=128)[:, :, rows],
                                  ev[k][:, :CL].rearrange("p (cc t) -> p cc t", cc=CC), reads=[ev_b[k]])
                        else:
                            T.op("act", lambda h, k=k, pi=pi, wtot=wtot: h.activation(out=evf[k][:, :wtot], in_=ps(pi)[:, :wtot], func=AF.Copy),
                                 [PB[pi]], [evf_b[k]])
                            T.op("dve", lambda h, k=k: h.bn_stats(out=sm[k][:, 4:10], in_=evf[k][:, 0:128]), [evf_b[k]], [sm_b[k]])
                            T.op("dve", lambda h, k=k: h.bn_aggr(out=sm[k][:, 10:12], in_=sm[k][:, 4:10]), [sm_b[k]], [sm_b[k]])
                            T.op("dve", lambda h, k=k: h.tensor_scalar(out=sm[k][:, 12:13], in0=sm[k][:, 11:12], scalar1=float(LN_EPS), scalar2=None, op0=ALU.add),
                                 [sm_b[k]], [sm_b[k]])
                            T.op("act", lambda h, k=k: h.activation(out=sm[k][:, 12:13], in_=sm[k][:, 12:13], func=AF.Sqrt), [sm_b[k]], [sm_b[k]])
                            T.op("dve", lambda h, k=k: h.reciprocal(out=sm[k][:, 13:14], in_=sm[k][:, 12:13]), [sm_b[k]], [sm_b[k]])
                            T.op("dve", lambda h, k=k: h.tensor_scalar(out=sq[:, 0:128], in0=evf[k][:, 0:128], scalar1=sm[k][:, 10:11], scalar2=sm[k][:, 13:14],
                                                                       op0=ALU.subtract, op1=ALU.mult), [evf_b[k], sm_b[k]], [sq_b])
                            T.op("dve", lambda h: h.tensor_tensor(out=sq[:, 0:128], in0=sq[:, 0:128], in1=ikg[:], op=ALU.mult), [sq_b, pb2], [sq_b])
                            T.op("dve", lambda h, k=k: h.tensor_tensor(out=tr[k][:, 0:128], in0=sq[:, 0:128], in1=ikb[:], op=ALU.add), [sq_b, pb3], [tr_b[k]])
                            pt = 6 + (tb % 2)
                            T.op("pe", lambda h, k=k, pt=pt: h.transpose(out=ps_bf(pt)[:, 0:128], in_=tr[k][:, 0:128], identity=identb[:]),
                                 [tr_b[k], identb_b], [PB[pt]])
                            T.op("act", lambda h, k=k, pt=pt: h.activation(out=ev[k][:, 0:128], in_=ps_bf(pt)[:, 0:128], func=AF.Copy), [PB[pt]], [ev_b[k]])
                            T.dma(kiT[:, rows], ev[k][:, 0:128], reads=[ev_b[k]])
                            wsc = float((128 ** -0.5) * (NI ** -0.5))
                            T.op("act", lambda h, k=k: h.activation(out=sq[:, 128:128 + NI], in_=evf[k][:, 128:128 + NI], func=AF.Abs, scale=wsc),
                                 [evf_b[k]], [sq_b])
                            T.op("act", lambda h, k=k: h.activation(out=sq[:, 256:256 + NI], in_=evf[k][:, 128:128 + NI], func=AF.Sign), [evf_b[k]], [sq_b])
                            T.dma(wia_d[rows, :], sq[:, 128:128 + NI], reads=[sq_b])
                            T.dma(wis_d[rows, :], sq[:, 256:256 + NI], reads=[sq_b])
                            fo = 128 + NI
                            T.op("dve", lambda h, k=k: h.tensor_tensor(out=sq[:, 384:384 + NHB], in0=evf[k][:, fo:fo + NHB], in1=bft[:], op=ALU.add),
                                 [evf_b[k], pb4], [sq_b])
                            T.op("act", lambda h: h.activation(out=sq[:, 384:384 + NHB], in_=sq[:, 384:384 + NHB], func=AF.Exp, scale=-1.0), [sq_b], [sq_b])
                            T.op("act", lambda h: h.activation(out=sq[:, 384:384 + NHB], in_=sq[:, 384:384 + NHB], func=AF.Ln, bias=1.0), [sq_b], [sq_b])
                            T.op("dve", lambda h: h.tensor_scalar(out=sq[:, 448:448 + NHB], in0=sq[:, 384:384 + NHB], scalar1=-1.0, scalar2=None, op0=ALU.mult),
                                 [sq_b], [sq_b])
                            T.dma(lf_d[rows, :], sq[:, 448:448 + NHB], reads=[sq_b])
            drain_eops(len(eops))
            end_phase()

    def indexer_setup(pes, cs, cb):
        if True:
            ident_f = cs[:, 0, :]
            negadm = cs[:, 1, :]
            negoth = cs[:, 2, :]
            identb = sb(pes, nc, "identb", [128, 128], BF16)
            identb_b = Buf("identb")
            T.op("dve", lambda h: h.tensor_copy(out=identb[:], in_=ident_f), [cb], [identb_b])
            ki = sb(pes, nc, "ki", [128, S], BF16)
            ki_b = Buf("ki")
            T.dma(ki[:], kiT, writes=[ki_b])
            qi = [sb(pes, nc, "qi%d" % i, [128, NI, 128], BF16) for i in range(2)]
            qi_b = [Buf("qi%d" % i) for i in range(2)]
            wa = [sb(pes, nc, "wa%d" % i, [128, 2 * NI], F32) for i in range(2)]
            wa_b = [Buf("wa%d" % i) for i in range(2)]
            acc = sb(pes, nc, "acc", [128, S], F32)
            acc_b = Buf("acc")
            junk = sb(pes, nc, "junk", [128, S], BF16)
            junk_b = Buf("junk")
            rl = [sb(pes, nc, "rl%d" % i, [128, 512], F32) for i in range(3)]
            rl_b = [Buf("rl%d" % i) for i in range(3)]
            bs = sb(pes, nc, "bs", [128, 16], F32)
            bs_b = Buf("bs")
            m01 = sb(pes, nc, "m01", [128, S], BF16)
            m01_b = Buf("m01")
            mT = [sb(pes, nc, "mT%d" % i, [128, 8, 128], BF16) for i in range(2)]
            mT_b = [Buf("mT%d" % i) for i in range(2)]
            rcnt = [0]
            mcnt = [0]
            qiv = qiT.rearrange("(h d) s -> d h s", d=128)

            def load_q(j):
                qs = slice(j * 128, (j + 1) * 128)
                i2 = j % 2
                for a0 in range(0, NI, 8):
                    a1 = min(NI, a0 + 8)
                    T.dma(qi[i2][:, a0:a1, :], qiv[:, a0:a1, qs], writes=[qi_b[i2]])
                T.dma(wa[i2][:, 0:NI], wia_d[qs, :], writes=[wa_b[i2]])
                T.dma(wa[i2][:, NI:2 * NI], wis_d[qs, :], writes=[wa_b[i2]])

            load_q(0)

            def qb(j):
                i2 = j % 2
                if j + 1 < NTO:
                    load_q(j + 1)
                n1 = (j + 1) * 128
                nk = 2 * n1
                first = True
                for (kbase, abase) in ((0, 0), (SO, n1)):
                    for c0 in range(0, n1, 512):
                        cw = min(512, n1 - c0)
                        for hi in range(NI):
                            pi = hi % 4
                            T.op("pe", lambda h, pi=pi, hi=hi, c0=c0, cw=cw, i2=i2, kbase=kbase: h.matmul(
                                out=ps(pi)[:, :cw], lhsT=qi[i2][:, hi, :], rhs=ki[:, kbase + c0:kbase + c0 + cw], start=True, stop=True),
                                [qi_b[i2], ki_b], [PB[pi]])
                            k = rcnt[0] % 3
                            rcnt[0] += 1
                            T.op("act", lambda h, k=k, pi=pi, cw=cw, hi=hi, i2=i2: h.activation(
                                out=rl[k][:, :cw], in_=ps(pi)[:, :cw], func=AF.Relu, scale=wa[i2][:, hi:hi + 1]),
                                [PB[pi], wa_b[i2]], [rl_b[k]])
                            a0 = abase + c0
                            if hi == 0:
                                T.op("dve", lambda h, k=k, a0=a0, cw=cw, i2=i2: h.tensor_scalar(
                                    out=acc[:, a0:a0 + cw], in0=rl[k][:, :cw], scalar1=wa[i2][:, NI:NI + 1], scalar2=None, op0=ALU.mult),
                                    [rl_b[k], wa_b[i2]], [acc_b])
                            else:
                                T.op("dve", lambda h, k=k, a0=a0, cw=cw, hi=hi, i2=i2: h.scalar_tensor_tensor(
                                    out=acc[:, a0:a0 + cw], in0=rl[k][:, :cw], scalar=wa[i2][:, NI + hi:NI + hi + 1], in1=acc[:, a0:a0 + cw],
                                    op0=ALU.mult, op1=ALU.add), [rl_b[k], wa_b[i2], acc_b], [acc_b])
                d0 = j * 128
                d1 = n1 + j * 128
                T.op("dve", lambda h, d0=d0: h.tensor_tensor(out=acc[:, d0:d0 + 128], in0=acc[:, d0:d0 + 128], in1=negadm, op=ALU.add), [acc_b, cb], [acc_b])
                T.op("dve", lambda h, d1=d1: h.tensor_tensor(out=acc[:, d1:d1 + 128], in0=acc[:, d1:d1 + 128], in1=negoth, op=ALU.add), [acc_b, cb], [acc_b])
                NIT = 18
                T.op("dve", lambda h: h.memset(bs[:, 2:3], 0.0), [], [bs_b])
                for it in range(1, NIT + 1):
                    w_i = 32.0 / (2.0 ** it)
                    T.op("dve", lambda h, nk=nk: h.tensor_scalar(out=junk[:, :nk], in0=acc[:, :nk], scalar1=bs[:, 2:3], scalar2=None,
                                                                 op0=ALU.is_ge, op1=ALU.add, accum_out=bs[:, 3:4]), [acc_b, bs_b], [junk_b, bs_b])
                    T.op("dve", lambda h, w_i=w_i: h.tensor_scalar(out=bs[:, 4:5], in0=bs[:, 3:4], scalar1=float(TOPK), scalar2=float(w_i),
                                                                   op0=ALU.is_ge, op1=ALU.mult), [bs_b], [bs_b])
                    if it < NIT:
                        T.op("dve", lambda h, w_i=w_i: h.scalar_tensor_tensor(out=bs[:, 2:3], in0=bs[:, 4:5], scalar=float(-w_i / 2.0), in1=bs[:, 2:3],
                                                                              op0=ALU.add, op1=ALU.add), [bs_b], [bs_b])
                    else:
                        T.op("dve", lambda h, w_i=w_i: h.scalar_tensor_tensor(out=bs[:, 0:1], in0=bs[:, 4:5], scalar=float(-w_i), in1=bs[:, 2:3],
                                                                              op0=ALU.add, op1=ALU.add), [bs_b], [bs_b])
                T.op("dve", lambda h, nk=nk: h.tensor_scalar(out=m01[:, :nk], in0=acc[:, :nk], scalar1=bs[:, 0:1], scalar2=None, op0=ALU.is_ge),
                     [acc_b, bs_b], [m01_b])
                for (abase, sbase) in ((0, 0), (j + 1, NTO)):
                    for k0 in range(0, j + 1, 8):
                        k1 = min(j + 1, k0 + 8)
                        pt = 4 + (mcnt[0] % 2)
                        mi = mcnt[0] % 2
                        mcnt[0] += 1
                        for kb in range(k0, k1):
                            ab = abase + kb
                            T.op("pe", lambda h, kb=kb, k0=k0, pt=pt, ab=ab: h.transpose(out=ps_bf(pt)[:, (kb - k0) * 128:(kb - k0 + 1) * 128],
                                                                                       in_=m01[:, ab * 128:(ab + 1) * 128], identity=identb[:]),
                                 [m01_b, identb_b], [PB[pt]])
                        n = (k1 - k0) * 128
                        T.op("act", lambda h, mi=mi, pt=pt, n=n, k0=k0, k1=k1: h.activation(
                            out=mT[mi][:, 0:k1 - k0, :], in_=ps_bf(pt)[:, :n].rearrange("p (a b) -> p a b", a=k1 - k0), func=AF.Copy),
                            [PB[pt]], [mT_b[mi]])
                        T.dma(maskT_d[j * NT + sbase + k0:j * NT + sbase + k1].rearrange("k p t -> p k t"), mT[mi][:, 0:k1 - k0, :], reads=[mT_b[mi]])
            return qb

    def phase_attn(kinds=("dsa", "fox"), with_idx=False):
        with ExitStack() as pes:
            cs, cb = load_consts(pes)
            idx_qb = indexer_setup(pes, cs, cb) if with_idx else None
            has_dsa = "dsa" in kinds
            ident_f = cs[:, 0, :]
            admT = cs[:, 3, :]
            bmTs = (cs[:, 4, :], cs[:, 5, :], cs[:, 6, :])
            caus = cs[:, 7, :]
            caus_o = cs[:, 8, :]
            utri = cs[:, 9, :]
            sel0 = cs[:, 10, :]
            cmat = cs[:, 11, :]
            identb = sb(pes, nc, "identb", [128, 128], BF16)
            identb_b = Buf("identb")
            T.op("dve", lambda h: h.tensor_copy(out=identb[:], in_=ident_f), [cb], [identb_b])
            causb = sb(pes, nc, "causb", [128, 2, 128], BF16)
            causb_b = Buf("causb")
            T.op("dve", lambda h: h.tensor_copy(out=causb[:, 0, :], in_=caus), [cb], [causb_b])
            T.op("dve", lambda h: h.tensor_copy(out=causb[:, 1, :], in_=caus_o), [cb], [causb_b])
            En_b = Buf("En")
            if has_dsa:
                En = sb(pes, nc, "En", [128, NHA, 3, 128], BF16)
                T.dma(En[:].rearrange("p a b c -> p (a b c)"), En_d, writes=[En_b])

            kT_s = [sb(pes, nc, "kT%d" % i, [128, S], BF16) for i in range(2)]
            kT_sb = [Buf("kT%d" % i) for i in range(2)]
            qT_s = [sb(pes, nc, "qT%d" % i, [128, SO], BF16) for i in range(2)]
            qT_sb = [Buf("qT%d" % i) for i in range(2)]
            V_s = [sb(pes, nc, "V%d" % i, [128, NT, 132], BF16) for i in range(2)]
            V_sb = [Buf("V%d" % i) for i in range(2)]
            for i_ in range(2):
                T.op("dve", lambda h, i_=i_: h.memset(V_s[i_][:, :, 128:132], 1.0), [], [V_sb[i_]])
            NG = 5
            eb = [sb(pes, nc, "eb%d" % i, [128, 512], BF16) for i in range(NG)]
            eb_b = [Buf("eb%d" % i) for i in range(NG)]
            pb = [sb(pes, nc, "pb%d" % i, [128, 512], BF16) for i in range(NG)]
            pb_b = [Buf("pb%d" % i) for i in range(NG)]
            mk = [sb(pes, nc, "mk%d" % i, [128, NT, 128], BF16) for i in range(4 if has_dsa else 0)]
            mk_b = [Buf("mk%d" % i) for i in range(4)]
            Ff = [sb(pes, nc, "Ff%d" % i, [128, NT], F32) for i in range(4)]
            Ff_b = [Buf("Ff%d" % i) for i in range(4)]
            ob = [sb(pes, nc, "ob%d" % i, [128, 128], BF16) for i in range(2)]
            ob_b = [Buf("ob%d" % i) for i in range(2)]
            rc = [sb(pes, nc, "rc%d" % i, [128, 2], F32) for i in range(2)]
            rc_b = [Buf("rc%d" % i) for i in range(2)]
            otb = [sb(pes, nc, "otb%d" % i, [128, 128], BF16) for i in range(2)]
            otb_b = [Buf("otb%d" % i) for i in range(2)]
            gcnt = [0]
            ocnt = [0]
            scale = float(128 ** -0.5)
            LA = 3

            def run_head(kind, hh, hd_row, slot, fbias=None, fb_b=None):
                kT, kT_b, qT, qT_b, V, V_b = kT_s[slot], kT_sb[slot], qT_s[slot], qT_sb[slot], V_s[slot], V_sb[slot]
                tasks = []
                for g in range(NTO):
                    kbl = list(range(0, g + 1)) + list(range(NTO, NTO + g + 1))
                    nkb = len(kbl)
                    oi = ocnt[0] % 2
                    ocnt[0] += 1
                    grp = [(k0, min(nkb, k0 + 4)) for k0 in range(0, nkb, 4)]
                    for ii, (k0, k1) in enumerate(grp):
                        gi = gcnt[0] % NG
                        psid = gcnt[0] % 4
                        gcnt[0] += 1
                        tasks.append(dict(g=g, nkb=nkb, k0=k0, k1=k1, first=(ii == 0), last=(ii == len(grp) - 1), oi=oi, po=6 + oi,
                                          gi=gi, psid=psid, qi=g % 4, kbl=kbl))

                def qk(t):
                    g, k0, k1, psid, nkb, qi = t["g"], t["k0"], t["k1"], t["psid"], t["nkb"], t["qi"]
                    kbl = t["kbl"]
                    if t["first"]:
                        if kind == "dsa":
                            for sbase in (0, NTO):
                                for a0 in range(0, g + 1, 8):
                                    a1 = min(g + 1, a0 + 8)
                                    T.dma(mk[qi][:, sbase + a0:sbase + a1, :],
                                          maskT_d[g * NT + sbase + a0:g * NT + sbase + a1].rearrange("k p t -> p k t"), writes=[mk_b[qi]])
                        else:
                            pass
                    for ki_ in range(k0, k1):
                        kb = kbl[ki_]
                        T.op("pe", lambda h, kb=kb, ki_=ki_, k0=k0, psid=psid, g=g: h.matmul(
                            out=ps(psid)[:, (ki_ - k0) * 128:(ki_ - k0 + 1) * 128], lhsT=kT[:, kb * 128:(kb + 1) * 128],
                            rhs=qT[:, g * 128:(g + 1) * 128], start=True, stop=True), [kT_b, qT_b], [PB[psid]])

                def soft(t):
                    g, k0, k1, psid, gi, qi, kbl = t["g"], t["k0"], t["k1"], t["psid"], t["gi"], t["qi"], t["kbl"]
                    n = (k1 - k0) * 128
                    if kind == "fox":
                        for ki_ in range(k0, k1):
                            kb = kbl[ki_]
                            o0 = (ki_ - k0) * 128
                            T.op("act", lambda h, gi=gi, psid=psid, o0=o0, kb=kb, g=g: h.activation(
                                out=pb[gi][:, o0:o0 + 128], in_=ps(psid)[:, o0:o0 + 128], func=AF.Exp, scale=scale, bias=fbias[:, g, kb:kb + 1]),
                                [PB[psid], fb_b], [pb_b[gi]])
                            nj = 0 if kb == g else (1 if kb == NTO + g else None)
                            if nj is not None:
                                T.op("dve", lambda h, gi=gi, o0=o0, nj=nj: h.tensor_tensor(
                                    out=pb[gi][:, o0:o0 + 128], in0=pb[gi][:, o0:o0 + 128], in1=causb[:, nj, :], op=ALU.mult),
                                    [pb_b[gi], causb_b], [pb_b[gi]])
                        return
                    T.op("act", lambda h, gi=gi, psid=psid, n=n: h.activation(out=eb[gi][:, :n], in_=ps(psid)[:, :n], func=AF.Exp, scale=scale),
                         [PB[psid]], [eb_b[gi]])
                    runs = []
                    for ki_ in range(k0, k1):
                        if runs and kbl[ki_] == runs[-1][1] + runs[-1][2]:
                            runs[-1][2] += 1
                        else:
                            runs.append([ki_, kbl[ki_], 1])
                    for (ki0, kb0, cnt_) in runs:
                        o0 = (ki0 - k0) * 128
                        src = mk[qi][:, kb0:kb0 + cnt_, :] if kind == "dsa" else Ff[qi][:, kb0:kb0 + cnt_].unsqueeze(2).to_broadcast([128, cnt_, 128])
                        srcb = mk_b[qi] if kind == "dsa" else Ff_b[qi]
                        T.op("dve", lambda h, gi=gi, o0=o0, cnt_=cnt_, src=src: h.tensor_tensor(
                            out=pb[gi][:, o0:o0 + cnt_ * 128].rearrange("p (a b) -> p a b", a=cnt_),
                            in0=eb[gi][:, o0:o0 + cnt_ * 128].rearrange("p (a b) -> p a b", a=cnt_), in1=src, op=ALU.mult),
                            [eb_b[gi], srcb], [pb_b[gi]])
                    for ki_ in range(k0, k1):
                        kb = kbl[ki_]
                        o0 = (ki_ - k0) * 128
                        if kind == "dsa":
                            nj = 0 if kb == g else (1 if kb == NTO + g else (2 if kb == NTO + g - 1 else None))
                            if nj is not None:
                                T.op("dve", lambda h, gi=gi, o0=o0, nj=nj: h.tensor_tensor(
                                    out=pb[gi][:, o0:o0 + 128], in0=pb[gi][:, o0:o0 + 128], in1=En[:, hh, nj, :], op=ALU.mult),
                                    [pb_b[gi], En_b], [pb_b[gi]])
                        else:
                            nj = 0 if kb == g else (1 if kb == NTO + g else None)
                            if nj is not None:
                                T.op("dve", lambda h, gi=gi, o0=o0, nj=nj: h.tensor_tensor(
                                    out=pb[gi][:, o0:o0 + 128], in0=pb[gi][:, o0:o0 + 128], in1=causb[:, nj, :], op=ALU.mult),
                                    [pb_b[gi], causb_b], [pb_b[gi]])

                def pv(t):
                    g, k0, k1, gi, po, nkb, oi, kbl = t["g"], t["k0"], t["k1"], t["gi"], t["po"], t["nkb"], t["oi"], t["kbl"]
                    for ki_ in range(k0, k1):
                        kb = kbl[ki_]
                        T.op("pe", lambda h, gi=gi, kb=kb, ki_=ki_, k0=k0, po=po, nkb=nkb: h.matmul(
                            out=ps(po)[:, :130], lhsT=pb[gi][:, (ki_ - k0) * 128:(ki_ - k0 + 1) * 128], rhs=V[:, kb, 0:130],
                            start=(ki_ == 0), stop=(ki_ == nkb - 1)), [pb_b[gi], V_b], [PB[po]])
                    if t["last"]:
                        T.op("dve", lambda h, oi=oi, po=po: h.reciprocal(out=rc[oi][:, 0:1], in_=ps(po)[:, 128:129]), [PB[po]], [rc_b[oi]])
                        T.op("dve", lambda h, oi=oi, po=po: h.tensor_scalar(out=ob[oi][:], in0=ps(po)[:, 0:128], scalar1=rc[oi][:, 0:1], scalar2=None,
                                                                          op0=ALU.mult), [PB[po], rc_b[oi]], [ob_b[oi]])
                        pt = 4 + oi
                        T.op("pe", lambda h, oi=oi, pt=pt: h.transpose(out=ps_bf(pt)[:, 0:128], in_=ob[oi][:], identity=identb[:]),
                             [ob_b[oi], identb_b], [PB[pt]])
                        T.op("act", lambda h, oi=oi, pt=pt: h.activation(out=otb[oi][:], in_=ps_bf(pt)[:, 0:128], func=AF.Copy), [PB[pt]], [otb_b[oi]])
                        T.dma(oT[hd_row:hd_row + 128, g * 128:(g + 1) * 128], otb[oi][:], reads=[otb_b[oi]])

                for idx in range(len(tasks) + LA):
                    if idx < len(tasks):
                        qk(tasks[idx])
                    j = idx - LA
                    if j >= 0:
                        soft(tasks[j])
                        pv(tasks[j])

            lf = sb(pes, nc, "lf", [128, NT, NHB], F32)
            lf_b = Buf("lf")
            for a0 in range(0, NT, 8):
                a1 = min(NT, a0 + 8)
                T.dma(lf[:, a0:a1, :], lf_d.rearrange("(kb p) h -> p kb h", p=128)[:, a0:a1, :], writes=[lf_b])
            cum = sb(pes, nc, "cum", [128, NT, NHB], F32)
            cum_b = Buf("cum")
            cref = sb(pes, nc, "cref", [128, NT, NHB], F32)
            cref_b = Buf("cref")
            NW = NT * NHB
            for c0 in range(0, NW, 512):
                cw = min(512, NW - c0)
                lfv = lf[:].rearrange("p a b -> p (a b)")
                T.op("pe", lambda h, c0=c0, cw=cw, lfv=lfv: h.matmul(out=ps(0)[:, :cw], lhsT=utri, rhs=lfv[:, c0:c0 + cw], start=True, stop=True), [cb, lf_b], [PB[0]])
                T.op("dve", lambda h, c0=c0, cw=cw: h.tensor_copy(out=cum[:].rearrange("p a b -> p (a b)")[:, c0:c0 + cw], in_=ps(0)[:, :cw]), [PB[0]], [cum_b])
            totT = sb(pes, nc, "totT", [128, NHB], F32)
            totT_b = Buf("totT")
            onesc = sb(pes, nc, "onesc", [128, 128], F32)
            onesc_b = Buf("onesc")
            T.op("pool", lambda h: h.memset(onesc[:], 1.0), [], [onesc_b])
            for hh in range(NHB):
                T.op("pe", lambda h, hh=hh: h.matmul(out=ps(3)[:NT, hh:hh + 1], lhsT=lf[:, :, hh], rhs=onesc[:, 0:1], start=True, stop=True),
                     [lf_b, onesc_b], [PB[3]])
            T.op("dve", lambda h: h.tensor_copy(out=totT[:NT, :], in_=ps(3)[:NT, :NHB]), [PB[3]], [totT_b])
            Rm = sb(pes, nc, "Rm", [128, NT, NHB], F32)
            Rm_b = Buf("Rm")
            T.op("dve", lambda h: h.tensor_tensor(out=Rm[:NT, :, :], in0=cmat[:NT, :NT].unsqueeze(2).to_broadcast([NT, NT, NHB]),
                                                  in1=totT[:NT, :].unsqueeze(1).to_broadcast([NT, NT, NHB]), op=ALU.mult), [cb, totT_b], [Rm_b])
            for c0 in range(0, NW, 512):
                cw = min(512, NW - c0)
                rv = Rm[:].rearrange("p a b -> p (a b)")
                T.op("pe", lambda h, c0=c0, cw=cw, rv=rv: h.matmul(out=ps(1)[:, :cw], lhsT=onesc[:NT, :], rhs=rv[:NT, c0:c0 + cw], start=True, stop=True),
                     [onesc_b, Rm_b], [PB[1]])
                T.op("dve", lambda h, c0=c0, cw=cw: h.tensor_tensor(out=cum[:].rearrange("p a b -> p (a b)")[:, c0:c0 + cw],
                                                                   in0=cum[:].rearrange("p a b -> p (a b)")[:, c0:c0 + cw], in1=ps(1)[:, :cw], op=ALU.add),
                     [PB[1], cum_b], [cum_b])
            for c0 in range(0, NW, 512):
                cw = min(512, NW - c0)
                cv = cum[:].rearrange("p a b -> p (a b)")
                T.op("pe", lambda h, c0=c0, cw=cw, cv=cv: h.matmul(out=ps(2)[:, :cw], lhsT=sel0, rhs=cv[:, c0:c0 + cw], start=True, stop=True), [cb, cum_b], [PB[2]])
                T.op("dve", lambda h, c0=c0, cw=cw: h.tensor_copy(out=cref[:].rearrange("p a b -> p (a b)")[:, c0:c0 + cw], in_=ps(2)[:, :cw]), [PB[2]], [cref_b])
            ck_b, ukf_b, ukT_b, uvf_b, uvb_b = Buf("ck"), Buf("ukf"), Buf("ukT"), Buf("uvf"), Buf("uvb")
            if has_dsa:
                ck = sb(pes, nc, "ck", [128, CC, S], BF16)
                T.dma(ck[:], ckvT.rearrange("(cc p) s -> p cc s", p=128), writes=[ck_b])
                ukf = sb(pes, nc, "ukf", [128, CL], F32)
                ukT = sb(pes, nc, "ukT", [128, CC, 128], BF16)
                uvf = sb(pes, nc, "uvf", [128, CC, 128], F32)
                uvb = sb(pes, nc, "uvb", [128, CC, 128], BF16)
            def prep_dsa(hh, slot):
                kT, kT_b, qT, qT_b, V, V_b = kT_s[slot], kT_sb[slot], qT_s[slot], qT_sb[slot], V_s[slot], V_sb[slot]
                T.dma(ukf[:], w_uk[hh], writes=[ukf_b])
                for cc in range(CC):
                    T.op("pe", lambda h, cc=cc: h.transpose(out=ps(5)[:, cc * 128:(cc + 1) * 128], in_=ukf[:, cc * 128:(cc + 1) * 128], identity=ident_f),
                         [ukf_b, cb], [PB[5]])
                T.op("act", lambda h: h.activation(out=ukT[:], in_=ps(5)[:, :CC * 128].rearrange("p (a b) -> p a b", a=CC), func=AF.Copy), [PB[5]], [ukT_b])
                T.dma(uvf[:], w_uv[hh].rearrange("(cc p) d -> p cc d", p=128), writes=[uvf_b])
                T.op("dve", lambda h: h.tensor_copy(out=uvb[:], in_=uvf[:]), [uvf_b], [uvb_b])
                T.dma(qT[:, :SO], qaT[hh * 128:(hh + 1) * 128, :], writes=[qT_b])
                for c0 in range(0, S, 512):
                    pi = (c0 // 512) % 4
                    cw = min(512, S - c0)
                    for cc in range(CC):
                        T.op("pe", lambda h, pi=pi, cc=cc, c0=c0, cw=cw: h.matmul(out=ps(pi)[:, :cw], lhsT=ukT[:, cc, :], rhs=ck[:, cc, c0:c0 + cw],
                                                                                start=(cc == 0), stop=(cc == CC - 1)), [ukT_b, ck_b], [PB[pi]])
                    T.op("act", lambda h, pi=pi, c0=c0, cw=cw: h.activation(out=kT[:, c0:c0 + cw], in_=ps(pi)[:, :cw], func=AF.Copy), [PB[pi]], [kT_b])
                for kb0 in range(0, NT, 4):
                    kb1 = min(NT, kb0 + 4)
                    pi = 4 + ((kb0 // 4) % 2)
                    for kb in range(kb0, kb1):
                        for cc in range(CC):
                            T.op("pe", lambda h, pi=pi, kb=kb, kb0=kb0, cc=cc: h.matmul(
                                out=ps(pi)[:, (kb - kb0) * 128:(kb - kb0 + 1) * 128], lhsT=ck[:, cc, kb * 128:(kb + 1) * 128], rhs=uvb[:, cc, :],
                                start=(cc == 0), stop=(cc == CC - 1)), [ck_b, uvb_b], [PB[pi]])
                    T.op("dve", lambda h, pi=pi, kb0=kb0, kb1=kb1: h.tensor_copy(
                        out=V[:, kb0:kb1, 0:128], in_=ps(pi)[:, :(kb1 - kb0) * 128].rearrange("p (a b) -> p a b", a=kb1 - kb0)), [PB[pi]], [V_b])

            fbias_s = [sb(pes, nc, "fbias%d" % i, [128, NTO, NT], F32) for i in range(2)]
            fb_sb = [Buf("fbias%d" % i) for i in range(2)]

            def prep_fox(hh, slot):
                kT, kT_b, qT, qT_b, V, V_b = kT_s[slot], kT_sb[slot], qT_s[slot], qT_sb[slot], V_s[slot], V_sb[slot]
                fbias, fb_b = fbias_s[slot], fb_sb[slot]
                for g in range(NTO):
                    T.op("dve", lambda h, g=g, hh=hh: h.tensor_scalar(out=fbias[:, g, :], in0=cum[:, :, hh], scalar1=-1.0, scalar2=cref[:, g, hh:hh + 1],
                                                                     op0=ALU.mult, op1=ALU.add), [cum_b, cref_b], [fb_b])
                T.op("dve", lambda h: h.tensor_scalar(out=fbias[:], in0=fbias[:], scalar1=75.0, scalar2=None, op0=ALU.min), [fb_b], [fb_b])
                T.dma(kT[:], kbT[hh * 128:(hh + 1) * 128, :], writes=[kT_b])
                T.dma(qT[:, :SO], qbT[hh * 128:(hh + 1) * 128, :], writes=[qT_b])
                for a0 in range(0, NT, 8):
                    a1 = min(NT, a0 + 8)
                    T.dma(V[:, a0:a1, 0:128], vb_d.rearrange("(kb p) w -> p kb w", p=128)[:, a0:a1, hh * 128:(hh + 1) * 128], writes=[V_b])

            heads = [hd for hd in ([("dsa", hh) for hh in range(NHA)] + [("fox", hh) for hh in range(NHB)]) if hd[0] in kinds]

            def prep(i):
                kind, hh = heads[i]
                if kind == "dsa":
                    prep_dsa(hh, i % 2)
                else:
                    prep_fox(hh, i % 2)

            def run_i(i):
                kind, hh = heads[i]
                if i + 1 < len(heads):
                    prep(i + 1)
                if kind == "dsa":
                    run_head("dsa", hh, hh * 128, i % 2)
                else:
                    run_head("fox", hh, WA + hh * 128, i % 2, fbias_s[i % 2], fb_sb[i % 2])

            nh = len(heads)
            done_h = 0
            if nh:
                prep(0)
            if idx_qb is not None:
                for j in range(NTO):
                    idx_qb(j)
                    while done_h < (j + 1) * nh // NTO:
                        run_i(done_h)
                        done_h += 1
            while done_h < nh:
                run_i(done_h)
                done_h += 1
            end_phase()

    def phase_wout():
        if not bg.done():
            with ExitStack() as pes:
                bg.engs = ["dve", "act"]
                bg.lq, bg.sq = None, None
                bg.attach(pes, "bgw")
                bg.run_all()
                bg.flush_inflight()
                end_phase()
        with ExitStack() as pes:
            oTs = sb(pes, nc, "oTs", [128, KC, TT], BF16)
            oTs_b = Buf("oTs")
            NWB = 2
            wb_t = [sb(pes, nc, "wb%d" % i, [128, KC * DNW], BF16) for i in range(NWB)]
            wb_b = [Buf("wb%d" % i) for i in range(NWB)]
            xp = [sb(pes, nc, "xp%d" % i, [128, DNW], F32) for i in range(2)]
            xp_b = [Buf("xp%d" % i) for i in range(2)]
            rp = [sb(pes, nc, "rp%d" % i, [128, DNW], F32) for i in range(2)]
            rp_b = [Buf("rp%d" % i) for i in range(2)]
            wcnt = [0]
            pcnt = [0]
            for t0 in range(0, SO, TT):
                for a0 in range(0, KC, 8):
                    a1 = min(KC, a0 + 8)
                    T.dma(oTs[:, a0:a1, :], oT.rearrange("(kc p) s -> p kc s", p=128)[:, a0:a1, t0:t0 + TT], writes=[oTs_b])
                for n in range(NDN):
                    wi = wcnt[0] % NWB
                    wcnt[0] += 1
                    T.dma(wb_t[wi][:], wout_s[n], writes=[wb_b[wi]])
                    wv = wb_t[wi][:].rearrange("p (kc f) -> p kc f", kc=KC)
                    for tb in range(TBT):
                        pi = pcnt[0] % 4
                        k = pcnt[0] % 2
                        pcnt[0] += 1
                        rows = slice(t0 + tb * 128, t0 + (tb + 1) * 128)
                        for kc in range(KC):
                            T.op("pe", lambda h, pi=pi, kc=kc, tb=tb, wv=wv: h.matmul(
                                out=ps(pi)[:, :DNW], lhsT=oTs[:, kc, tb * 128:(tb + 1) * 128], rhs=wv[:, kc, :],
                                start=(kc == 0), stop=(kc == KC - 1)), [oTs_b, wb_b[wi]], [PB[pi]])
                        T.dma(xp[k][:], h1[rows, n * DNW:(n + 1) * DNW], writes=[xp_b[k]])
                        T.op("dve", lambda h, k=k, pi=pi: h.scalar_tensor_tensor(
                            out=rp[k][:], in0=xp[k][:], scalar=float(ALPHA), in1=ps(pi)[:, :DNW], op0=ALU.mult, op1=ALU.add),
                            [xp_b[k], PB[pi]], [rp_b[k]])
                        T.dma(r_pre[rows, n * DNW:(n + 1) * DNW], rp[k][:], reads=[rp_b[k]])
            end_phase()

    phases = [("precast", phase_precast), ("ffn1", lambda: phase_ffn(0, x, S)), ("ln1", lambda: phase_ln(0, h1, S)),
              ("win", phase_win), ("idxfox", lambda: phase_attn(("fox",), True)), ("attn", lambda: phase_attn(("dsa",), False)), ("wout", phase_wout),
              ("ln2", lambda: phase_ln(1, h2, SO)), ("ffn2", lambda: phase_ffn(1, h2, SO)), ("ln3", lambda: phase_ln(2, out, SO))]
    dbg_out = {"ln1": h1, "ln2": h2}
    for nm, fn in phases:
        fn()
        if stop_after == nm:
            if nm in dbg_out:
                with ExitStack() as pes:
                    t_ = sb(pes, nc, "dbg", [128, D], F32)
                    tb_ = Buf("dbg")
                    for tb in range(NTO):
                        T.dma(t_[:], dbg_out[nm][tb * 128:(tb + 1) * 128, :], writes=[tb_])
                        T.dma(out[tb * 128:(tb + 1) * 128, :], t_[:], reads=[tb_])
                    end_phase()
            if nm == "attn":
                with ExitStack() as pes:
                    t1 = sb(pes, nc, "dbg1", [128, SO], BF16)
                    t2 = sb(pes, nc, "dbg2", [128, SO], F32)
                    b1, b2 = Buf("dbg1"), Buf("dbg2")
                    ov = out.rearrange("s d -> (s d)").rearrange("(a b) -> a b", b=SO)
                    for kc in range(KC):
                        T.dma(t1[:], oT[kc * 128:(kc + 1) * 128, :], writes=[b1])
                        T.op("dve", lambda h: h.tensor_copy(out=t2[:], in_=t1[:]), [b1], [b2])
                        T.dma(ov[kc * 128:(kc + 1) * 128, :], t2[:], reads=[b2])
                    end_phase()
            break
    es.close()
    return nc


def _bucket(rel):
    nb = NBK // 2
    me = nb // 2
    ret = np.where(rel > 0, nb, 0)
    n = np.abs(rel)
    nf = np.maximum(n, 1).astype(np.float32)
    large = me + (np.log(nf / np.float32(me)) / np.float32(math.log(128 / me)) * np.float32(nb - me)).astype(np.int32)
    large = np.minimum(large, nb - 1)
    return ret + np.where(n < me, n, large)


def host_consts(r, NT):
    NTO = NT // 2
    c = np.zeros((128, 13, 128), np.float32)
    p = np.arange(128)
    c[:, 0, :] = np.eye(128, dtype=np.float32)
    t = p[:, None]
    s = p[None, :]
    c[:, 1, :] = np.where((s // 64) <= (t // 64), 0.0, NEGBIG)
    c[:, 2, :] = 0.0 if r == 1 else NEGBIG
    sT = p[:, None]
    tT = p[None, :]
    c[:, 3, :] = ((sT // 64) <= (tT // 64)).astype(np.float32)
    c[:, 4, :] = _bucket(sT - tT).astype(np.float32)
    d_oj = -1 if r == 1 else 1
    d_ojm1 = -3 if r == 1 else -1
    c[:, 5, :] = _bucket(sT + d_oj * 128 - tT).astype(np.float32)
    c[:, 6, :] = _bucket(sT + d_ojm1 * 128 - tT).astype(np.float32)
    c[:, 7, :] = (sT <= tT).astype(np.float32)
    c[:, 8, :] = 1.0 if r == 1 else 0.0
    c[:, 9, :] = (p[:, None] <= p[None, :]).astype(np.float32)
    c[:, 10, :] = (p[:, None] == 0).astype(np.float32) * np.ones((1, 128), np.float32)
    glob = np.array([2 * lb + r if lb < NTO else 2 * (lb - NTO) + (1 - r) for lb in range(NT)])
    cm = (glob[:, None] < glob[None, :]).astype(np.float32)
    c[:NT, 11, :NT] = cm
    return c


def block_order(r, NT):
    own = [g for g in range(NT) if g % 2 == r]
    oth = [g for g in range(NT) if g % 2 != r]
    return own, oth


def make_in_map(inputs, b, r, cfg):
    c = derive(cfg)
    NT, D = c["NT"], c["D"]
    f = lambda a: np.ascontiguousarray(np.asarray(a, dtype=np.float32))
    own, oth = block_order(r, NT)
    xb = np.asarray(inputs["x"][b], dtype=np.float32).reshape(NT, 128, D)
    xp = np.ascontiguousarray(xb[own + oth].reshape(NT * 128, D))
    m = {
        "x": xp,
        "ffn1_w_gate": f(inputs["ffn1_w_gate"][0]), "ffn1_w_up": f(inputs["ffn1_w_up"][0]), "ffn1_w_down": f(inputs["ffn1_w_down"][0]),
        "ffn2_w_gate": f(inputs["ffn2_w_gate"][0]), "ffn2_w_up": f(inputs["ffn2_w_up"][0]), "ffn2_w_down": f(inputs["ffn2_w_down"][0]),
        "ln1_g": f(inputs["ln1_g"]), "ln1_b": f(inputs["ln1_b"]), "ln2_g": f(inputs["ln2_g"]), "ln2_b": f(inputs["ln2_b"]),
        "ln3_g": f(inputs["ln3_g"]), "ln3_b": f(inputs["ln3_b"]),
        "w_in": f(inputs["w_in"][0]), "b_f": f(inputs["b_f"]), "kv_norm_g": f(inputs["kv_norm_g"]),
        "idx_k_g": f(inputs["idx_k_g"]), "idx_k_b": f(inputs["idx_k_b"]),
        "w_uk": f(inputs["w_uk"][0]), "w_uv": f(inputs["w_uv"][0]),
        "rel_bias": f(np.asarray(inputs["rel_bias"]).reshape(1, -1)),
        "w_out": f(inputs["w_out"][0]), "consts": host_consts(r, NT),
    }
    return m


def run(inputs, cfg, stop_after=None):
    c = derive(cfg)
    NT, D, S = c["NT"], c["D"], c["S"]
    nb = np.asarray(inputs["x"]).shape[0]
    n_cores = 2 * nb
    nc = build_program(cfg, stop_after=stop_after)
    in_maps = [make_in_map(inputs, cidx // 2, cidx % 2, cfg) for cidx in range(n_cores)]
    res = run_bass_kernel_spmd(nc, in_maps, core_ids=list(range(n_cores)))
    if stop_after is not None:
        return [np.asarray(res.results[cidx]["out"]) for cidx in range(n_cores)]
    out = np.zeros((nb, NT, 128, D), np.float32)
    for cidx in range(n_cores):
        own, _ = block_order(cidx % 2, NT)
        out[cidx // 2, own] = np.asarray(res.results[cidx]["out"], dtype=np.float32).reshape(NT // 2, 128, D)
    return out.reshape(nb, S, D)


def kernel(**inputs):
    return run(inputs, FULL_CFG)
```

```python
import math
import numpy as np
from contextlib import ExitStack
import concourse.bass as bass
import concourse.mybir as mybir
from concourse.bass_utils import run_bass_kernel_spmd

F32 = mybir.dt.float32
BF16 = mybir.dt.bfloat16
AF = mybir.ActivationFunctionType
ALU = mybir.AluOpType
AX = mybir.AxisListType

FULL_CFG = dict(D=4096, FF=11008, S=4096, CL=512, NI=32, TOPK=256, TT=512)
ALPHA = 2.0 ** 0.25
LN_EPS = 1e-5
RMS_EPS = 1e-6
NBK = 32
NEGBIG = -1.0e30
STORE_Q = "pool"


def derive(cfg):
    c = dict(cfg)
    D = c["D"]
    c["KC"] = D // 128
    c["NFC"] = c["FF"] // 128
    c["WA"] = D // 2
    c["WB"] = D // 2
    c["NHA"] = c["WA"] // 128
    c["NHB"] = c["WB"] // 128
    c["NT"] = c["S"] // 128
    c["CC"] = c["CL"] // 128
    NI = c["NI"]
    off = {}
    o = 0
    for nm, w in [("qa", c["WA"]), ("ckv", c["CL"]), ("qi", NI * 128), ("ki", 128), ("wi", NI),
                  ("qb", c["WB"]), ("kb", c["WB"]), ("vb", c["WB"]), ("fb", c["NHB"])]:
        off[nm] = (o, w)
        o += w
    c["off"] = off
    c["DIN"] = o
    fm = []
    for nm in ("qa", "qi", "qb", "kb"):
        s, w = off[nm]
        for i in range(w // 128):
            fm.append((nm, i, s + i * 128))
    c["fm"] = fm
    tm = [[("ckv", off["ckv"][0], c["CL"], 0)],
          [("ki", off["ki"][0], 128, 0), ("wi", off["wi"][0], NI, 128), ("fb", off["fb"][0], c["NHB"], 128 + NI)]]
    vb0, vbw = off["vb"]
    o = 0
    while o < vbw:
        w = min(512, vbw - o)
        tm.append([("vb", vb0 + o, w, 0, o)])
        o += w
    c["tm"] = tm
    c["NDN"] = max(1, D // 512)
    c["DNW"] = min(512, D)
    return c


class Buf:
    __slots__ = ("name", "w", "r", "lsem", "ssem")

    def __init__(self, name):
        self.name = name
        self.w = None
        self.r = {}
        self.lsem = None
        self.ssem = None


class Trk:
    ENG = ("pe", "act", "dve", "pool", "sp")

    def __init__(self, nc, es, n_dma_sems=26):
        self.nc = nc
        self.semobj = {}
        self.semval = {}
        for e in self.ENG:
            self.semobj[e] = es.enter_context(nc.semaphore("sem_" + e))
            self.semval[e] = 0
        self.dma_pool = []
        for i in range(n_dma_sems):
            k = "d%d" % i
            self.semobj[k] = es.enter_context(nc.semaphore("sem_" + k))
            self.semval[k] = 0
            self.dma_pool.append(k)
        self.free_dma = list(self.dma_pool)
        self.waited = {e: {} for e in self.ENG}
        self.prog = {e: [] for e in self.ENG}
        self.phase_dma = set()
        self.store_q = STORE_Q

    def new_phase(self):
        self.store_q = STORE_Q
        self.free_dma = list(self.dma_pool)
        self.phase_dma = set()

    def _dsem(self):
        k = self.free_dma.pop(0)
        self.phase_dma.add(k)
        return k

    def _wait(self, eng, deps):
        wd = self.waited[eng]
        best = {}
        for (k, v) in deps:
            if k == eng and eng in ("pe", "sp"):
                continue
            if wd.get(k, 0) >= v:
                continue
            if best.get(k, 0) < v:
                best[k] = v
        for k, v in best.items():
            wd[k] = v
            so = self.semobj[k]
            self.prog[eng].append(lambda h, so=so, v=v: h.wait_ge(so, v))

    def _deps(self, reads, writes):
        deps = []
        for b in reads:
            if b.w is not None:
                deps.append(b.w)
        for b in writes:
            if b.w is not None:
                deps.append(b.w)
            deps.extend(b.r.items())
        return deps

    def op(self, eng, fn, reads=(), writes=()):
        self._wait(eng, self._deps(reads, writes))
        self.semval[eng] += 1
        v = self.semval[eng]
        so = self.semobj[eng]
        self.prog[eng].append(lambda h, fn=fn, so=so: fn(h).then_inc(so, 1))
        for b in writes:
            b.w = (eng, v)
            b.r = {}
        for b in reads:
            if b.r.get(eng, 0) < v:
                b.r[eng] = v

    def dma(self, out_ap, in_ap, reads=(), writes=(), q=None, **kw):
        if q is None:
            q = "sp" if writes else self.store_q
        if writes:
            b = writes[0]
            if b.lsem is None:
                b.lsem = self._dsem()
            k = b.lsem
        else:
            b = reads[0]
            if b.ssem is None:
                b.ssem = self._dsem()
            k = b.ssem
        self._wait(q, [d for d in self._deps(reads, writes) if d[0] != k])
        self.semval[k] += 16
        v = self.semval[k]
        so = self.semobj[k]
        self.prog[q].append(lambda h, so=so, o=out_ap, i=in_ap, kw=kw: h.dma_start(out=o, in_=i, **kw).then_inc(so, 16))
        for b in writes:
            b.w = (k, v)
            b.r = {}
        for b in reads:
            if b.r.get(k, 0) < v:
                b.r[k] = v

    def drain_dma(self):
        deps = [(k, self.semval[k]) for k in self.phase_dma]
        self._wait("sp", deps)

    def emit_block(self):
        nc = self.nc
        progs = self.prog
        self.prog = {e: [] for e in self.ENG}
        with nc.Block() as block:
            @block.sync
            def _(h):
                for t in progs["sp"]:
                    t(h)

            @block.tensor
            def _(h):
                for t in progs["pe"]:
                    t(h)

            @block.scalar
            def _(h):
                for t in progs["act"]:
                    t(h)

            @block.vector
            def _(h):
                for t in progs["dve"]:
                    t(h)

            @block.gpsimd
            def _(h):
                for t in progs["pool"]:
                    t(h)


class Ctx:
    pass


_SBN = [0]


def sb(es, nc, name, shape, dt):
    _SBN[0] += 1
    return es.enter_context(nc.sbuf_tensor("%s_%d" % (name, _SBN[0]), list(shape), dt))


def build_program(cfg, stop_after=None):
    c = derive(cfg)
    D, FF, S, KC, NFC, NT, TT = c["D"], c["FF"], c["S"], c["KC"], c["NFC"], c["NT"], c["TT"]
    CL, CC, NI, NHA, NHB, WA, WB, DIN = c["CL"], c["CC"], c["NI"], c["NHA"], c["NHB"], c["WA"], c["WB"], c["DIN"]
    NDN, DNW = c["NDN"], c["DNW"]
    TOPK = c["TOPK"]
    fm, tm, off = c["fm"], c["tm"], c["off"]
    NFM = len(fm)
    NTM = len(tm)
    SL = 16
    NSL = (NFC + SL - 1) // SL
    TBT = TT // 128
    assert S % TT == 0
    EOPS_PER_SITE = cfg.get("EOPS_PER_SITE", 8)
    BG_IDX = cfg.get("BG_IDX", 4)
    BG_ATT = cfg.get("BG_ATT", 4)
    SO = S // 2
    NTO = NT // 2
    assert SO % TT == 0
    NCST = 13

    nc = bass.Bass("TRN2", target_bir_lowering=False)

    def din(name, shape, dt=F32):
        return nc.dram_tensor(name, list(shape), dt, kind="ExternalInput").ap()

    def dscr(name, shape, dt):
        return nc.dram_tensor(name, list(shape), dt, kind="Internal").ap()

    x = din("x", [S, D])
    w_g = [din("ffn1_w_gate", [D, FF]), din("ffn2_w_gate", [D, FF])]
    w_u = [din("ffn1_w_up", [D, FF]), din("ffn2_w_up", [D, FF])]
    w_d = [din("ffn1_w_down", [FF, D]), din("ffn2_w_down", [FF, D])]
    ln_g = [din("ln1_g", [1, D]), din("ln2_g", [1, D]), din("ln3_g", [1, D])]
    ln_b = [din("ln1_b", [1, D]), din("ln2_b", [1, D]), din("ln3_b", [1, D])]
    w_in = din("w_in", [D, DIN])
    b_f = din("b_f", [1, NHB])
    kv_g = din("kv_norm_g", [1, CL])
    ik_g = din("idx_k_g", [1, 128])
    ik_b = din("idx_k_b", [1, 128])
    w_uk = din("w_uk", [NHA, 128, CL])
    w_uv = din("w_uv", [NHA, CL, 128])
    rel_bias = din("rel_bias", [1, NBK * NHA])
    w_out = din("w_out", [D, D])
    cst = din("consts", [128, NCST, 128])
    out = nc.dram_tensor("out", [SO, D], F32, kind="ExternalOutput").ap()

    gu_s = [dscr("gu1_s", [NFC, 128, 2 * KC * 128], BF16), dscr("gu2_s", [NFC, 128, 2 * KC * 128], BF16)]
    d_s = [dscr("d1_s", [NDN * NSL, 128, SL * DNW], BF16), dscr("d2_s", [NDN * NSL, 128, SL * DNW], BF16)]
    win_fm = dscr("win_fm", [NFM, 128, KC * 128], BF16)
    win_tm = dscr("win_tm", [NTM, 128, KC * 512], BF16)
    wout_s = dscr("wout_s", [NDN, 128, KC * DNW], BF16)
    r_pre = dscr("r_pre", [S, D], F32)
    h1 = dscr("h1", [S, D], F32)
    h2 = dscr("h2", [SO, D], F32)
    qaT = dscr("qaT", [WA, SO], BF16)
    qiT = dscr("qiT", [NI * 128, SO], BF16)
    qbT = dscr("qbT", [WB, SO], BF16)
    kbT = dscr("kbT", [WB, S], BF16)
    ckvT = dscr("ckvT", [CL, S], BF16)
    kiT = dscr("kiT", [128, S], BF16)
    vb_d = dscr("vb_d", [S, WB], BF16)
    wia_d = dscr("wia_d", [S, NI], F32)
    wis_d = dscr("wis_d", [S, NI], F32)
    lf_d = dscr("lf_d", [S, NHB], F32)
    maskT_d = dscr("maskT_d", [NTO * NT, 128, 128], BF16)
    oT = dscr("oT", [D, SO], BF16)
    En_d = dscr("En_d", [128, NHA * 3 * 128], BF16)

    es = ExitStack()
    T = Trk(nc, es)
    psum = es.enter_context(nc.psum_tensor("psum", [128, 8 * 512], F32))
    PB = [Buf("ps%d" % i) for i in range(8)]

    def ps(i, n=512):
        return psum[:, i * 512:i * 512 + n]

    def ps_bf(i):
        return psum[:, i * 512:(i + 1) * 512].bitcast(BF16)

    cast_rr = [0]

    def cast(out_ap, in_ap, rd, wr, engs=("dve", "pool", "act")):
        e = engs[cast_rr[0] % len(engs)]
        cast_rr[0] += 1
        if e == "act":
            T.op("act", lambda h: h.activation(out=out_ap, in_=in_ap, func=AF.Copy), rd, wr)
        else:
            T.op(e, lambda h: h.tensor_copy(out=out_ap, in_=in_ap), rd, wr)

    def end_phase():
        T.drain_dma()
        T.emit_block()
        T.new_phase()

    def ffn_units(f, usz=4096):
        units = []
        for m, W in enumerate((w_g[f], w_u[f])):
            Wv = W.rearrange("(kc p) f -> p kc f", p=128)
            for fc in range(NFC):
                for k0 in range(0, KC, 8):
                    k1 = min(KC, k0 + 8)
                    units.append((Wv[:, k0:k1, fc * 128:(fc + 1) * 128],
                                  gu_s[f][fc, :, (m * KC + k0) * 128:(m * KC + k1) * 128], k1 - k0, 128, False))
        Wv = w_d[f].rearrange("(j p) d -> p j d", p=128)
        jstep = max(1, usz // DNW)
        for n in range(NDN):
            for q in range(NSL):
                j0 = q * SL
                j1 = min(NFC, j0 + SL)
                for ja in range(j0, j1, jstep):
                    jb = min(j1, ja + jstep)
                    units.append((Wv[:, ja:jb, n * DNW:(n + 1) * DNW],
                                  d_s[f][n * NSL + q, :, (ja - j0) * DNW:(jb - j0) * DNW], jb - ja, DNW, False))
        return units

    def win_units(usz=4096):
        units = []
        Wv = w_in.rearrange("(kc p) f -> p kc f", p=128)
        for i, (nm, ci, col) in enumerate(fm):
            for k0 in range(0, KC, 8):
                k1 = min(KC, k0 + 8)
                units.append((Wv[:, k0:k1, col:col + 128], win_fm[i, :, k0 * 128:k1 * 128], k1 - k0, 128, False))
        for i, segs in enumerate(tm):
            for seg in segs:
                col, w, dofs = seg[1], seg[2], seg[3]
                kstep = max(1, usz // w)
                dv = win_tm[i].rearrange("p (kc f) -> p kc f", f=512)
                for k0 in range(0, KC, kstep):
                    k1 = min(KC, k0 + kstep)
                    units.append((Wv[:, k0:k1, col:col + w], dv[:, k0:k1, dofs:dofs + w], k1 - k0, w, True))
        return units

    def wout_units(usz=4096):
        units = []
        Wv = w_out.rearrange("(kc p) f -> p kc f", p=128)
        kstep = max(1, usz // DNW)
        for n in range(NDN):
            for k0 in range(0, KC, kstep):
                k1 = min(KC, k0 + kstep)
                units.append((Wv[:, k0:k1, n * DNW:(n + 1) * DNW], wout_s[n, :, k0 * DNW:k1 * DNW], k1 - k0, DNW, False))
        return units

    class Repacker:
        def __init__(self, units, engs, lq=None, sq=None, nb=4, la=2, usz=4096):
            self.usz = usz
            self.units = units
            self.engs = engs
            self.lq, self.sq = lq, sq
            self.nb, self.la = nb, la
            self.nl = 0
            self.ncst = 0
            self.rr = 0
            self.att = False

        def attach(self, pes, tag):
            self.sin = [sb(pes, nc, "%s_in%d" % (tag, i), [128, self.usz], F32) for i in range(self.nb)]
            self.sout = [sb(pes, nc, "%s_out%d" % (tag, i), [128, self.usz], BF16) for i in range(self.nb)]
            self.bin = [Buf("rpi%d" % i) for i in range(self.nb)]
            self.bout = [Buf("rpo%d" % i) for i in range(self.nb)]
            self.att = True

        def _load(self):
            src_ap, dst_ap, a, b, dst3 = self.units[self.nl]
            bi = self.nl % self.nb
            T.dma(self.sin[bi][:, :a * b].rearrange("p (a b) -> p a b", a=a), src_ap, writes=[self.bin[bi]], q=self.lq)
            self.nl += 1

        def _finish(self):
            src_ap, dst_ap, a, b, dst3 = self.units[self.ncst]
            bi = self.ncst % self.nb
            n = a * b
            e = self.engs[self.rr % len(self.engs)]
            self.rr += 1
            o_ap, i_ap = self.sout[bi][:, :n], self.sin[bi][:, :n]
            if e == "act":
                T.op("act", lambda h: h.activation(out=o_ap, in_=i_ap, func=AF.Copy), [self.bin[bi]], [self.bout[bi]])
            else:
                T.op(e, lambda h: h.tensor_copy(out=o_ap, in_=i_ap), [self.bin[bi]], [self.bout[bi]])
            src = self.sout[bi][:, :n].rearrange("p (a b) -> p a b", a=a) if dst3 else self.sout[bi][:, :n]
            T.dma(dst_ap, src, reads=[self.bout[bi]], q=self.sq)
            self.ncst += 1

        def step(self, k=1):
            if not self.att:
                return
            for _ in range(k):
                if self.nl < len(self.units):
                    self._load()
                if self.ncst < self.nl and (self.nl - self.ncst > self.la or self.nl == len(self.units)):
                    self._finish()

        def flush_inflight(self):
            while self.ncst < self.nl:
                self._finish()
            self.att = False

        def run_all(self):
            while self.ncst < len(self.units):
                self.step(1)

        def done(self):
            return self.ncst == len(self.units)

    def phase_precast():
        with ExitStack() as pes:
            rp = Repacker(ffn_units(0), ["dve", "act", "dve", "act", "dve", "act", "pool"])
            rp.attach(pes, "p0")
            rp.run_all()
            rp.flush_inflight()
            end_phase()

    BGU = 1024
    bg = Repacker(win_units(BGU) + ffn_units(1, BGU) + wout_units(BGU), ["dve", "act"], lq="pool", sq="pool", nb=3, la=2, usz=BGU)

    def load_transpose(pes_bufs, src, t0, ntok, dstT, dstT_bufs, ident_f, ident_buf, stage, stage_bufs, ps_ids):
        for it in lt_items(src, t0, ntok, dstT, dstT_bufs, ident_f, ident_buf, stage, stage_bufs, ps_ids):
            it()

    def lt_items(src, t0, ntok, dstT, dstT_bufs, ident_f, ident_buf, stage, stage_bufs, ps_ids, dq=None):
        SW = stage[0].shape[1]
        KW = SW // 128
        pieces = [(tb, c0) for tb in range(ntok // 128) for c0 in range(0, D, SW)]
        ns = len(stage)

        def load(i):
            tb, c0 = pieces[i]
            si = i % ns
            T.dma(stage[si][:, :SW], src[t0 + tb * 128:t0 + (tb + 1) * 128, c0:c0 + SW], writes=[stage_bufs[si]], q=dq)

        def piece(i):
            tb, c0 = pieces[i]
            si = i % ns
            kbase = c0 // 128
            for k0 in range(0, KW, 4):
                k1 = min(KW, k0 + 4)
                pi = ps_ids[(k0 // 4) % len(ps_ids)]
                for kc in range(k0, k1):
                    T.op("pe", lambda h, kc=kc, pi=pi, k0=k0, si=si: h.transpose(
                        out=ps(pi)[:, (kc - k0) * 128:(kc - k0 + 1) * 128],
                        in_=stage[si][:, kc * 128:(kc + 1) * 128], identity=ident_f),
                        [stage_bufs[si], ident_buf], [PB[pi]])
                n = (k1 - k0) * 128
                e = "act" if (k0 // 4) % 2 == 0 else "dve"
                o_ap = dstT[:, kbase + k0:kbase + k1, tb * 128:(tb + 1) * 128]
                i_ap = ps(pi)[:, :n].rearrange("p (a b) -> p a b", a=k1 - k0)
                if e == "act":
                    T.op("act", lambda h, o_ap=o_ap, i_ap=i_ap: h.activation(out=o_ap, in_=i_ap, func=AF.Copy),
                         [PB[pi]], [dstT_bufs[tb]])
                else:
                    T.op("dve", lambda h, o_ap=o_ap, i_ap=i_ap: h.tensor_copy(out=o_ap, in_=i_ap),
                         [PB[pi]], [dstT_bufs[tb]])
            if i + ns < len(pieces):
                load(i + ns)

        items = [lambda: [load(i) for i in range(min(ns, len(pieces)))]]
        items += [lambda i=i: piece(i) for i in range(len(pieces))]
        return items

    def load_consts(pes):
        cs = sb(pes, nc, "cst", [128, NCST, 128], F32)
        cb = Buf("cst")
        T.dma(cs[:], cst, writes=[cb])
        return cs, cb

    def phase_ffn(f, src, ntok):
        with ExitStack() as pes:
            idt = sb(pes, nc, "idt", [128, 128], F32)
            cb = Buf("idt")
            T.dma(idt[:], cst[:, 0, :], writes=[cb])
            ident_f = idt[:]
            if f == 0:
                bg.attach(pes, "bgf")
            xT = sb(pes, nc, "xT", [128, KC, TT], BF16)
            xT_b = [Buf("xT%d" % i) for i in range(TBT)]
            aT = sb(pes, nc, "aT", [128, NFC, TT], BF16)
            aT_b = [Buf("aT%d" % i) for i in range(NFC)]
            NWB = 3
            WSZ = max(2 * KC * 128, SL * DNW)
            wb_t = [sb(pes, nc, "wb%d" % i, [128, WSZ], BF16) for i in range(NWB)]
            wb_b = [Buf("wb%d" % i) for i in range(NWB)]
            stage = [sb(pes, nc, "stg%d" % i, [128, D // 2], F32) for i in range(1)]
            stage_b = [Buf("stg%d" % i) for i in range(1)]
            sil = [sb(pes, nc, "sil%d" % i, [128, TT], BF16) for i in range(2)]
            sil_b = [Buf("sil%d" % i) for i in range(2)]
            xp = [sb(pes, nc, "xp%d" % i, [128, DNW], F32) for i in range(2)]
            xp_b = [Buf("xp%d" % i) for i in range(2)]
            rp = [sb(pes, nc, "rp%d" % i, [128, DNW], F32) for i in range(2)]
            rp_b = [Buf("rp%d" % i) for i in range(2)]
            wcnt = [0]
            pcnt = [0]
            nxt_items = lt_items(src, 0, TT, xT, xT_b, ident_f, cb, stage, stage_b, [2, 3], dq="act")
            for t0 in range(0, ntok, TT):
                for it in nxt_items:
                    it()
                nxt_items = lt_items(src, t0 + TT, TT, xT, xT_b, ident_f, cb, stage, stage_b, [2, 3], dq="act") if t0 + TT < ntok else []
                for fc in range(NFC):
                    if f == 0:
                        bg.step(3 if fc % 2 == 0 else 2)
                    wi = wcnt[0] % NWB
                    wcnt[0] += 1
                    T.dma(wb_t[wi][:, :2 * KC * 128], gu_s[f][fc], writes=[wb_b[wi]])
                    wv = wb_t[wi][:, :2 * KC * 128].rearrange("p (m kc f) -> p m kc f", m=2, kc=KC)
                    pg = 2 + (fc % 2) * 2
                    pu = pg + 1
                    for m, pi in ((0, pg), (1, pu)):
                        for kc in range(KC):
                            T.op("pe", lambda h, m=m, pi=pi, kc=kc, wv=wv: h.matmul(
                                out=ps(pi)[:, :TT], lhsT=wv[:, m, kc, :], rhs=xT[:, kc, :],
                                start=(kc == 0), stop=(kc == KC - 1)),
                                [wb_b[wi]] + xT_b, [PB[pi]])
                    si = fc % 2
                    T.op("act", lambda h, si=si, pg=pg: h.activation(out=sil[si][:], in_=ps(pg)[:, :TT], func=AF.Silu),
                         [PB[pg]], [sil_b[si]])
                    T.op("dve", lambda h, si=si, pu=pu, fc=fc: h.tensor_tensor(
                        out=aT[:, fc, :], in0=ps(pu)[:, :TT], in1=sil[si][:], op=ALU.mult),
                        [PB[pu], sil_b[si]], [aT_b[fc]])
                for n in range(NDN):
                    for q in range(NSL):
                        j0 = q * SL
                        j1 = min(NFC, j0 + SL)
                        wi = wcnt[0] % NWB
                        wcnt[0] += 1
                        T.dma(wb_t[wi][:, :(j1 - j0) * DNW], d_s[f][n * NSL + q, :, :(j1 - j0) * DNW], writes=[wb_b[wi]])
                        wv = wb_t[wi][:, :SL * DNW].rearrange("p (j d) -> p j d", j=SL)
                        for tb in range(TBT):
                            pi = tb if TBT <= 2 else (tb if tb < 2 else 4 + tb)
                            for j in range(j0, j1):
                                T.op("pe", lambda h, pi=pi, j=j, j0=j0, tb=tb, wv=wv: h.matmul(
                                    out=ps(pi)[:, :DNW], lhsT=aT[:, j, tb * 128:(tb + 1) * 128], rhs=wv[:, j - j0, :],
                                    start=(j == 0), stop=(j == NFC - 1)),
                                    [wb_b[wi], aT_b[j]], [PB[pi]])
                        if nxt_items and n >= 1:
                            nxt_items.pop(0)()
                    for tb in range(TBT):
                        pi = tb if TBT <= 2 else (tb if tb < 2 else 4 + tb)
                        k = pcnt[0] % 2
                        pcnt[0] += 1
                        rows = slice(t0 + tb * 128, t0 + (tb + 1) * 128)
                        T.dma(xp[k][:], src[rows, n * DNW:(n + 1) * DNW], writes=[xp_b[k]])
                        T.op("act", lambda h, k=k, pi=pi: h.activation(out=rp[k][:], in_=ps(pi)[:, :DNW], func=AF.Copy, scale=0.5),
                             [PB[pi]], [rp_b[k]])
                        T.op("dve", lambda h, k=k: h.scalar_tensor_tensor(
                            out=rp[k][:], in0=xp[k][:], scalar=float(ALPHA), in1=rp[k][:], op0=ALU.mult, op1=ALU.add),
                            [xp_b[k], rp_b[k]], [rp_b[k]])
                        T.dma(r_pre[rows, n * DNW:(n + 1) * DNW], rp[k][:], reads=[rp_b[k]])
            if f == 0:
                bg.flush_inflight()
            end_phase()

    def phase_ln(li, dst, ntok):
        if li == 0 and not bg.done():
            with ExitStack() as pes:
                bg.engs = ["dve", "act"]
                bg.lq, bg.sq = None, None
                bg.attach(pes, "bgl")
                bg.run_all()
                bg.flush_inflight()
                end_phase()
        with ExitStack() as pes:
            gt = sb(pes, nc, "ln_g", [128, D], F32)
            bt = sb(pes, nc, "ln_b", [128, D], F32)
            gb, bb = Buf("g"), Buf("b")
            T.dma(gt[:], ln_g[li].partition_broadcast(128), writes=[gb])
            T.dma(bt[:], ln_b[li].partition_broadcast(128), writes=[bb])
            NR = 5
            rows = [sb(pes, nc, "lnr%d" % i, [128, D], F32) for i in range(NR)]
            rows_b = [Buf("lnr%d" % i) for i in range(NR)]
            nst = (D + 511) // 512
            st = [sb(pes, nc, "lnst%d" % i, [128, nst * 6], F32) for i in range(NR)]
            mv = [sb(pes, nc, "lnmv%d" % i, [128, 4], F32) for i in range(NR)]
            sm_b = [Buf("lnsm%d" % i) for i in range(NR)]
            for tb in range(ntok // 128):
                i = tb % NR
                T.dma(rows[i][:], r_pre[tb * 128:(tb + 1) * 128, :], writes=[rows_b[i]])
                for s_ in range(nst):
                    w = min(512, D - s_ * 512)
                    T.op("dve", lambda h, i=i, s_=s_, w=w: h.bn_stats(out=st[i][:, s_ * 6:(s_ + 1) * 6], in_=rows[i][:, s_ * 512:s_ * 512 + w]),
                         [rows_b[i]], [sm_b[i]])
                T.op("dve", lambda h, i=i: h.bn_aggr(out=mv[i][:, 0:2], in_=st[i][:]), [sm_b[i]], [sm_b[i]])
                T.op("dve", lambda h, i=i: h.tensor_scalar(out=mv[i][:, 2:3], in0=mv[i][:, 1:2], scalar1=float(LN_EPS), scalar2=None, op0=ALU.add),
                     [sm_b[i]], [sm_b[i]])
                T.op("act", lambda h, i=i: h.activation(out=mv[i][:, 2:3], in_=mv[i][:, 2:3], func=AF.Sqrt), [sm_b[i]], [sm_b[i]])
                T.op("dve", lambda h, i=i: h.reciprocal(out=mv[i][:, 3:4], in_=mv[i][:, 2:3]), [sm_b[i]], [sm_b[i]])
                T.op("dve", lambda h, i=i: h.tensor_scalar(out=mv[i][:, 2:3], in0=mv[i][:, 0:1], scalar1=mv[i][:, 3:4], scalar2=-1.0,
                                                           op0=ALU.mult, op1=ALU.mult), [sm_b[i]], [sm_b[i]])
                T.op("act", lambda h, i=i: h.activation(out=rows[i][:], in_=rows[i][:], func=AF.Identity, scale=mv[i][:, 3:4], bias=mv[i][:, 2:3]),
                     [sm_b[i], rows_b[i]], [rows_b[i]])
                T.op("pool", lambda h, i=i: h.tensor_tensor(out=rows[i][:], in0=rows[i][:], in1=gt[:], op=ALU.mult), [rows_b[i], gb], [rows_b[i]])
                T.op("dve", lambda h, i=i: h.tensor_tensor(out=rows[i][:], in0=rows[i][:], in1=bt[:], op=ALU.add), [rows_b[i], bb], [rows_b[i]])
                T.dma(dst[tb * 128:(tb + 1) * 128, :], rows[i][:], reads=[rows_b[i]])
            end_phase()

    def phase_win():
        with ExitStack() as pes:
            cs, cb = load_consts(pes)
            ident_f = cs[:, 0, :]
            hT = sb(pes, nc, "hT", [128, KC, TT], BF16)
            hT_b = [Buf("hT%d" % i) for i in range(TBT)]
            stage = [sb(pes, nc, "stg%d" % i, [128, D], F32) for i in range(1)]
            stage_b = [Buf("stg%d" % i) for i in range(1)]
            NWB = 2
            wb_t = [sb(pes, nc, "wb%d" % i, [128, KC * 512], BF16) for i in range(NWB)]
            wb_b = [Buf("wb%d" % i) for i in range(NWB)]
            NFB = 4
            fb_t = [sb(pes, nc, "fmb%d" % i, [128, KC * 128], BF16) for i in range(NFB)]
            fb_b = [Buf("fmb%d" % i) for i in range(NFB)]
            fcnt = [0]
            ev = [sb(pes, nc, "ev%d" % i, [128, 512], BF16) for i in range(2)]
            ev_b = [Buf("ev%d" % i) for i in range(2)]
            evf = [sb(pes, nc, "evf%d" % i, [128, 512], F32) for i in range(2)]
            evf_b = [Buf("evf%d" % i) for i in range(2)]
            sq = sb(pes, nc, "sq", [128, 512], F32)
            sq_b = Buf("sq")
            sm = [sb(pes, nc, "wsm%d" % i, [128, 16], F32) for i in range(2)]
            sm_b = [Buf("wsm%d" % i) for i in range(2)]
            tr = [sb(pes, nc, "trn%d" % i, [128, 512], BF16) for i in range(2)]
            tr_b = [Buf("trn%d" % i) for i in range(2)]
            identb = sb(pes, nc, "identb", [128, 128], BF16)
            identb_b = Buf("identb")
            T.op("dve", lambda h: h.tensor_copy(out=identb[:], in_=ident_f), [cb], [identb_b])
            kvg = sb(pes, nc, "kvg", [128, CL], F32)
            ikg = sb(pes, nc, "ikg", [128, 128], F32)
            ikb = sb(pes, nc, "ikb", [128, 128], F32)
            bft = sb(pes, nc, "bft", [128, NHB], F32)
            par_b = Buf("par")
            T.dma(kvg[:], kv_g.partition_broadcast(128), writes=[par_b])
            pb2, pb3, pb4 = Buf("p2"), Buf("p3"), Buf("p4")
            T.dma(ikg[:], ik_g.partition_broadcast(128), writes=[pb2])
            T.dma(ikb[:], ik_b.partition_broadcast(128), writes=[pb3])
            T.dma(bft[:], b_f.partition_broadcast(128), writes=[pb4])
            eops = []

            def defer(*a_, **k_):
                eops.append(("op", a_, k_))

            def defer_dma(*a_, **k_):
                eops.append(("dma", a_, k_))

            def drain_eops(n):
                for _ in range(min(n, len(eops))):
                    kind_, a_, k_ = eops.pop(0)
                    if kind_ == "op":
                        T.op(*a_, **k_)
                    else:
                        T.dma(*a_, **k_)

            rbb = sb(pes, nc, "rbb", [128, NBK * NHA], F32)
            rbb_b = Buf("rbb")
            T.dma(rbb[:], rel_bias.partition_broadcast(128), writes=[rbb_b])
            Ef = sb(pes, nc, "Ef", [128, 3, 128], F32)
            Ef_b = Buf("Ef")
            oh = sb(pes, nc, "oh", [128, 3, 128], F32)
            oh_b = Buf("oh")
            En = sb(pes, nc, "En", [128, NHA, 3, 128], BF16)
            En_b = Buf("En")
            negc = sb(pes, nc, "negc", [128, NHA], F32)
            negc_b = Buf("negc")
            for hh in range(NHA):
                defer("dve", lambda h, hh=hh: h.tensor_scalar(out=negc[:, hh:hh + 1], in0=rbb[:, 15 * NHA + hh:15 * NHA + hh + 1], scalar1=-1.0, scalar2=None,
                                                             op0=ALU.mult), [rbb_b], [negc_b])
            for hh in range(NHA):
                for b in range(NBK):
                    for j, bm in enumerate((cs[:, 4, :], cs[:, 5, :], cs[:, 6, :])):
                        defer("dve", lambda h, b=b, j=j, bm=bm: h.tensor_scalar(out=oh[:, j, :], in0=bm, scalar1=float(b), scalar2=None, op0=ALU.is_equal),
                             [cb], [oh_b])
                        if b == 0:
                            defer("dve", lambda h, b=b, j=j, hh=hh: h.tensor_scalar(out=Ef[:, j, :], in0=oh[:, j, :], scalar1=rbb[:, b * NHA + hh:b * NHA + hh + 1],
                                                                                   scalar2=None, op0=ALU.mult), [oh_b, rbb_b], [Ef_b])
                        else:
                            defer("dve", lambda h, b=b, j=j, hh=hh: h.scalar_tensor_tensor(out=Ef[:, j, :], in0=oh[:, j, :], scalar=rbb[:, b * NHA + hh:b * NHA + hh + 1],
                                                                                          in1=Ef[:, j, :], op0=ALU.mult, op1=ALU.add), [oh_b, rbb_b, Ef_b], [Ef_b])
                defer("act", lambda h, hh=hh: h.activation(out=Ef[:, :, :], in_=Ef[:, :, :], func=AF.Exp, bias=negc[:, hh:hh + 1]), [Ef_b, negc_b], [Ef_b])
                defer("dve", lambda h, hh=hh: h.tensor_tensor(out=En[:, hh, 0, :], in0=Ef[:, 0, :], in1=cs[:, 3, :], op=ALU.mult), [Ef_b, cb], [En_b])
                defer("dve", lambda h, hh=hh: h.tensor_copy(out=En[:, hh, 1:3, :], in_=Ef[:, 1:3, :]), [Ef_b], [En_b])

            defer_dma(En_d, En[:].rearrange("p a b c -> p (a b c)"), reads=[En_b])
            wcnt = [0]
            ecnt = [0]
            dstmap = {"qa": qaT, "qi": qiT, "qb": qbT, "kb": kbT}
            for t0 in range(0, S, TT):
                load_transpose(None, h1, t0, TT, hT, hT_b, ident_f, cb, stage, stage_b, [0, 1])
                for i, (nm, ci, col) in enumerate(fm):
                    if t0 >= SO and nm != "kb":
                        continue
                    wi = fcnt[0] % NFB
                    fcnt[0] += 1
                    T.dma(fb_t[wi][:], win_fm[i], writes=[fb_b[wi]])
                    wv = fb_t[wi][:].rearrange("p (kc f) -> p kc f", kc=KC)
                    pi = 2 + (i % 2)
                    for kc in range(KC):
                        T.op("pe", lambda h, pi=pi, kc=kc, wv=wv: h.matmul(
                            out=ps(pi)[:, :TT], lhsT=wv[:, kc, :], rhs=hT[:, kc, :], start=(kc == 0), stop=(kc == KC - 1)),
                            [fb_b[wi]] + hT_b, [PB[pi]])
                    k = ecnt[0] % 2
                    ecnt[0] += 1
                    if k == 0:
                        T.op("act", lambda h, k=k, pi=pi: h.activation(out=ev[k][:, :TT], in_=ps(pi)[:, :TT], func=AF.Copy), [PB[pi]], [ev_b[k]])
                    else:
                        T.op("dve", lambda h, k=k, pi=pi: h.tensor_copy(out=ev[k][:, :TT], in_=ps(pi)[:, :TT]), [PB[pi]], [ev_b[k]])
                    T.dma(dstmap[nm][ci * 128:(ci + 1) * 128, t0:t0 + TT], ev[k][:, :TT], reads=[ev_b[k]])
                    drain_eops(EOPS_PER_SITE)
                for i, segs in enumerate(tm):
                    wi = wcnt[0] % NWB
                    wcnt[0] += 1
                    T.dma(wb_t[wi][:], win_tm[i], writes=[wb_b[wi]])
                    wv = wb_t[wi][:].rearrange("p (kc f) -> p kc f", kc=KC)
                    wtot = max(s_[3] + s_[2] for s_ in segs)
                    for tb in range(TBT):
                        rows = slice(t0 + tb * 128, t0 + (tb + 1) * 128)
                        pi = 4 + (tb % 2)
                        for kc in range(KC):
                            T.op("pe", lambda h, pi=pi, kc=kc, wv=wv, tb=tb, wtot=wtot: h.matmul(
                                out=ps(pi)[:, :wtot], lhsT=hT[:, kc, tb * 128:(tb + 1) * 128], rhs=wv[:, kc, :wtot],
                                start=(kc == 0), stop=(kc == KC - 1)), [wb_b[wi], hT_b[tb]], [PB[pi]])
                        k = ecnt[0] % 2
                        ecnt[0] += 1
                        nm = segs[0][0]
                        if nm == "vb":
                            w, vo = segs[0][2], segs[0][4]
                            T.op("act", lambda h, k=k, pi=pi, w=w: h.activation(out=ev[k][:, :w], in_=ps(pi)[:, :w], func=AF.Copy), [PB[pi]], [ev_b[k]])
                            T.dma(vb_d[rows, vo:vo + w], ev[k][:, :w], reads=[ev_b[k]])
                        elif nm == "ckv":
                            T.op("act", lambda h, k=k, pi=pi: h.activation(out=sq[:, :CL], in_=ps(pi)[:, :CL], func=AF.Square, accum_out=sm[k][:, 0:1]),
                                 [PB[pi]], [sq_b, sm_b[k]])
                            T.op("dve", lambda h, k=k: h.tensor_scalar(out=sm[k][:, 1:2], in0=sm[k][:, 0:1], scalar1=1.0 / CL, scalar2=float(RMS_EPS),
                                                                       op0=ALU.mult, op1=ALU.add), [sm_b[k]], [sm_b[k]])
                            T.op("act", lambda h, k=k: h.activation(out=sm[k][:, 1:2], in_=sm[k][:, 1:2], func=AF.Sqrt), [sm_b[k]], [sm_b[k]])
                            T.op("dve", lambda h, k=k: h.reciprocal(out=sm[k][:, 2:3], in_=sm[k][:, 1:2]), [sm_b[k]], [sm_b[k]])
                            T.op("dve", lambda h, k=k, pi=pi: h.scalar_tensor_tensor(out=tr[k][:, :CL], in0=ps(pi)[:, :CL], scalar=sm[k][:, 2:3], in1=kvg[:],
                                                                                 op0=ALU.mult, op1=ALU.mult), [PB[pi], sm_b[k], par_b], [tr_b[k]])
                            pt = 6 + (tb % 2)
                            for cc in range(CC):
                                T.op("pe", lambda h, k=k, cc=cc, pt=pt: h.transpose(out=ps_bf(pt)[:, cc * 128:(cc + 1) * 128],
                                                                                  in_=tr[k][:, cc * 128:(cc + 1) * 128], identity=identb[:]),
                                     [tr_b[k], identb_b], [PB[pt]])
                            T.op("act", lambda h, k=k, pt=pt: h.activation(out=ev[k][:, :CL], in_=ps_bf(pt)[:, :CL], func=AF.Copy), [PB[pt]], [ev_b[k]])
                            T.dma(ckvT.rearrange("(cc p) s -> p cc s", p=128)[:, :, rows],
                                  ev[k][:, :CL].rearrange("p (cc t) -> p cc t", cc=CC), reads=[ev_b[k]])
                        else:
                            T.op("act", lambda h, k=k, pi=pi, wtot=wtot: h.activation(out=evf[k][:, :wtot], in_=ps(pi)[:, :wtot], func=AF.Copy),
                                 [PB[pi]], [evf_b[k]])
                            T.op("dve", lambda h, k=k: h.bn_stats(out=sm[k][:, 4:10], in_=evf[k][:, 0:128]), [evf_b[k]], [sm_b[k]])
                            T.op("dve", lambda h, k=k: h.bn_aggr(out=sm[k][:, 10:12], in_=sm[k][:, 4:10]), [sm_b[k]], [sm_b[k]])
                            T.op("dve", lambda h, k=k: h.tensor_scalar(out=sm[k][:, 12:13], in0=sm[k][:, 11:12], scalar1=float(LN_EPS), scalar2=None, op0=ALU.add),
                                 [sm_b[k]], [sm_b[k]])
                            T.op("act", lambda h, k=k: h.activation(out=sm[k][:, 12:13], in_=sm[k][:, 12:13], func=AF.Sqrt), [sm_b[k]], [sm_b[k]])
                            T.op("dve", lambda h, k=k: h.reciprocal(out=sm[k][:, 13:14], in_=sm[k][:, 12:13]), [sm_b[k]], [sm_b[k]])
                            T.op("dve", lambda h, k=k: h.tensor_scalar(out=sq[:, 0:128], in0=evf[k][:, 0:128], scalar1=sm[k][:, 10:11], scalar2=sm[k][:, 13:14],
                                                                       op0=ALU.subtract, op1=ALU.mult), [evf_b[k], sm_b[k]], [sq_b])
                            T.op("dve", lambda h: h.tensor_tensor(out=sq[:, 0:128], in0=sq[:, 0:128], in1=ikg[:], op=ALU.mult), [sq_b, pb2], [sq_b])
                            T.op("dve", lambda h, k=k: h.tensor_tensor(out=tr[k][:, 0:128], in0=sq[:, 0:128], in1=ikb[:], op=ALU.add), [sq_b, pb3], [tr_b[k]])
                            pt = 6 + (tb % 2)
                            T.op("pe", lambda h, k=k, pt=pt: h.transpose(out=ps_bf(pt)[:, 0:128], in_=tr[k][:, 0:128], identity=identb[:]),
                                 [tr_b[k], identb_b], [PB[pt]])
                            T.op("act", lambda h, k=k, pt=pt: h.activation(out=ev[k][:, 0:128], in_=ps_bf(pt)[:, 0:128], func=AF.Copy), [PB[pt]], [ev_b[k]])
                            T.dma(kiT[:, rows], ev[k][:, 0:128], reads=[ev_b[k]])
                            wsc = float((128 ** -0.5) * (NI ** -0.5))
                            T.op("act", lambda h, k=k: h.activation(out=sq[:, 128:128 + NI], in_=evf[k][:, 128:128 + NI], func=AF.Abs, scale=wsc),
                                 [evf_b[k]], [sq_b])
                            T.op("act", lambda h, k=k: h.activation(out=sq[:, 256:256 + NI], in_=evf[k][:, 128:128 + NI], func=AF.Sign), [evf_b[k]], [sq_b])
                            T.dma(wia_d[rows, :], sq[:, 128:128 + NI], reads=[sq_b])
                            T.dma(wis_d[rows, :], sq[:, 256:256 + NI], reads=[sq_b])
                            fo = 128 + NI
                            T.op("dve", lambda h, k=k: h.tensor_tensor(out=sq[:, 384:384 + NHB], in0=evf[k][:, fo:fo + NHB], in1=bft[:], op=ALU.add),
                                 [evf_b[k], pb4], [sq_b])
                            T.op("act", lambda h: h.activation(out=sq[:, 384:384 + NHB], in_=sq[:, 384:384 + NHB], func=AF.Exp, scale=-1.0), [sq_b], [sq_b])
                            T.op("act", lambda h: h.activation(out=sq[:, 384:384 + NHB], in_=sq[:, 384:384 + NHB], func=AF.Ln, bias=1.0), [sq_b], [sq_b])
                            T.op("dve", lambda h: h.tensor_scalar(out=sq[:, 448:448 + NHB], in0=sq[:, 384:384 + NHB], scalar1=-1.0, scalar2=None, op0=ALU.mult),
                                 [sq_b], [sq_b])
                            T.dma(lf_d[rows, :], sq[:, 448:448 + NHB], reads=[sq_b])
            drain_eops(len(eops))
            end_phase()

    def indexer_setup(pes, cs, cb):
        if True:
            ident_f = cs[:, 0, :]
            negadm = cs[:, 1, :]
            negoth = cs[:, 2, :]
            identb = sb(pes, nc, "identb", [128, 128], BF16)
            identb_b = Buf("identb")
            T.op("dve", lambda h: h.tensor_copy(out=identb[:], in_=ident_f), [cb], [identb_b])
            ki = sb(pes, nc, "ki", [128, S], BF16)
            ki_b = Buf("ki")
            T.dma(ki[:], kiT, writes=[ki_b])
            qi = [sb(pes, nc, "qi%d" % i, [128, NI, 128], BF16) for i in range(2)]
            qi_b = [Buf("qi%d" % i) for i in range(2)]
            wa = [sb(pes, nc, "wa%d" % i, [128, 2 * NI], F32) for i in range(2)]
            wa_b = [Buf("wa%d" % i) for i in range(2)]
            acc = sb(pes, nc, "acc", [128, S], F32)
            acc_b = Buf("acc")
            junk = sb(pes, nc, "junk", [128, S], BF16)
            junk_b = Buf("junk")
            rl = [sb(pes, nc, "rl%d" % i, [128, 512], F32) for i in range(3)]
            rl_b = [Buf("rl%d" % i) for i in range(3)]
            bs = sb(pes, nc, "bs", [128, 16], F32)
            bs_b = Buf("bs")
            m01 = sb(pes, nc, "m01", [128, S], BF16)
            m01_b = Buf("m01")
            mT = [sb(pes, nc, "mT%d" % i, [128, 8, 128], BF16) for i in range(2)]
            mT_b = [Buf("mT%d" % i) for i in range(2)]
            rcnt = [0]
            mcnt = [0]
            qiv = qiT.rearrange("(h d) s -> d h s", d=128)

            def load_q(j):
                qs = slice(j * 128, (j + 1) * 128)
                i2 = j % 2
                for a0 in range(0, NI, 8):
                    a1 = min(NI, a0 + 8)
                    T.dma(qi[i2][:, a0:a1, :], qiv[:, a0:a1, qs], writes=[qi_b[i2]])
                T.dma(wa[i2][:, 0:NI], wia_d[qs, :], writes=[wa_b[i2]])
                T.dma(wa[i2][:, NI:2 * NI], wis_d[qs, :], writes=[wa_b[i2]])

            load_q(0)

            def qb(j):
                i2 = j % 2
                if j + 1 < NTO:
                    load_q(j + 1)
                n1 = (j + 1) * 128
                nk = 2 * n1
                first = True
                for (kbase, abase) in ((0, 0), (SO, n1)):
                    for c0 in range(0, n1, 512):
                        cw = min(512, n1 - c0)
                        for hi in range(NI):
                            pi = hi % 4
                            T.op("pe", lambda h, pi=pi, hi=hi, c0=c0, cw=cw, i2=i2, kbase=kbase: h.matmul(
                                out=ps(pi)[:, :cw], lhsT=qi[i2][:, hi, :], rhs=ki[:, kbase + c0:kbase + c0 + cw], start=True, stop=True),
                                [qi_b[i2], ki_b], [PB[pi]])
                            k = rcnt[0] % 3
                            rcnt[0] += 1
                            T.op("act", lambda h, k=k, pi=pi, cw=cw, hi=hi, i2=i2: h.activation(
                                out=rl[k][:, :cw], in_=ps(pi)[:, :cw], func=AF.Relu, scale=wa[i2][:, hi:hi + 1]),
                                [PB[pi], wa_b[i2]], [rl_b[k]])
                            a0 = abase + c0
                            if hi == 0:
                                T.op("dve", lambda h, k=k, a0=a0, cw=cw, i2=i2: h.tensor_scalar(
                                    out=acc[:, a0:a0 + cw], in0=rl[k][:, :cw], scalar1=wa[i2][:, NI:NI + 1], scalar2=None, op0=ALU.mult),
                                    [rl_b[k], wa_b[i2]], [acc_b])
                            else:
                                T.op("dve", lambda h, k=k, a0=a0, cw=cw, hi=hi, i2=i2: h.scalar_tensor_tensor(
                                    out=acc[:, a0:a0 + cw], in0=rl[k][:, :cw], scalar=wa[i2][:, NI + hi:NI + hi + 1], in1=acc[:, a0:a0 + cw],
                                    op0=ALU.mult, op1=ALU.add), [rl_b[k], wa_b[i2], acc_b], [acc_b])
                d0 = j * 128
                d1 = n1 + j * 128
                T.op("dve", lambda h, d0=d0: h.tensor_tensor(out=acc[:, d0:d0 + 128], in0=acc[:, d0:d0 + 128], in1=negadm, op=ALU.add), [acc_b, cb], [acc_b])
                T.op("dve", lambda h, d1=d1: h.tensor_tensor(out=acc[:, d1:d1 + 128], in0=acc[:, d1:d1 + 128], in1=negoth, op=ALU.add), [acc_b, cb], [acc_b])
                NIT = 18
                T.op("dve", lambda h: h.memset(bs[:, 2:3], 0.0), [], [bs_b])
                for it in range(1, NIT + 1):
                    w_i = 32.0 / (2.0 ** it)
                    T.op("dve", lambda h, nk=nk: h.tensor_scalar(out=junk[:, :nk], in0=acc[:, :nk], scalar1=bs[:, 2:3], scalar2=None,
                                                                 op0=ALU.is_ge, op1=ALU.add, accum_out=bs[:, 3:4]), [acc_b, bs_b], [junk_b, bs_b])
                    T.op("dve", lambda h, w_i=w_i: h.tensor_scalar(out=bs[:, 4:5], in0=bs[:, 3:4], scalar1=float(TOPK), scalar2=float(w_i),
                                                                   op0=ALU.is_ge, op1=ALU.mult), [bs_b], [bs_b])
                    if it < NIT:
                        T.op("dve", lambda h, w_i=w_i: h.scalar_tensor_tensor(out=bs[:, 2:3], in0=bs[:, 4:5], scalar=float(-w_i / 2.0), in1=bs[:, 2:3],
                                                                              op0=ALU.add, op1=ALU.add), [bs_b], [bs_b])
                    else:
                        T.op("dve", lambda h, w_i=w_i: h.scalar_tensor_tensor(out=bs[:, 0:1], in0=bs[:, 4:5], scalar=float(-w_i), in1=bs[:, 2:3],
                                                                              op0=ALU.add, op1=ALU.add), [bs_b], [bs_b])
                T.op("dve", lambda h, nk=nk: h.tensor_scalar(out=m01[:, :nk], in0=acc[:, :nk], scalar1=bs[:, 0:1], scalar2=None, op0=ALU.is_ge),
                     [acc_b, bs_b], [m01_b])
                for (abase, sbase) in ((0, 0), (j + 1, NTO)):
                    for k0 in range(0, j + 1, 8):
                        k1 = min(j + 1, k0 + 8)
                        pt = 4 + (mcnt[0] % 2)
                        mi = mcnt[0] % 2
                        mcnt[0] += 1
                        for kb in range(k0, k1):
                            ab = abase + kb
                            T.op("pe", lambda h, kb=kb, k0=k0, pt=pt, ab=ab: h.transpose(out=ps_bf(pt)[:, (kb - k0) * 128:(kb - k0 + 1) * 128],
                                                                                       in_=m01[:, ab * 128:(ab + 1) * 128], identity=identb[:]),
                                 [m01_b, identb_b], [PB[pt]])
                        n = (k1 - k0) * 128
                        T.op("act", lambda h, mi=mi, pt=pt, n=n, k0=k0, k1=k1: h.activation(
                            out=mT[mi][:, 0:k1 - k0, :], in_=ps_bf(pt)[:, :n].rearrange("p (a b) -> p a b", a=k1 - k0), func=AF.Copy),
                            [PB[pt]], [mT_b[mi]])
                        T.dma(maskT_d[j * NT + sbase + k0:j * NT + sbase + k1].rearrange("k p t -> p k t"), mT[mi][:, 0:k1 - k0, :], reads=[mT_b[mi]])
            return qb

    def phase_attn(kinds=("dsa", "fox"), with_idx=False):
        with ExitStack() as pes:
            cs, cb = load_consts(pes)
            idx_qb = indexer_setup(pes, cs, cb) if with_idx else None
            has_dsa = "dsa" in kinds
            ident_f = cs[:, 0, :]
            admT = cs[:, 3, :]
            bmTs = (cs[:, 4, :], cs[:, 5, :], cs[:, 6, :])
            caus = cs[:, 7, :]
            caus_o = cs[:, 8, :]
            utri = cs[:, 9, :]
            sel0 = cs[:, 10, :]
            cmat = cs[:, 11, :]
            identb = sb(pes, nc, "identb", [128, 128], BF16)
            identb_b = Buf("identb")
            T.op("dve", lambda h: h.tensor_copy(out=identb[:], in_=ident_f), [cb], [identb_b])
            causb = sb(pes, nc, "causb", [128, 2, 128], BF16)
            causb_b = Buf("causb")
            T.op("dve", lambda h: h.tensor_copy(out=causb[:, 0, :], in_=caus), [cb], [causb_b])
            T.op("dve", lambda h: h.tensor_copy(out=causb[:, 1, :], in_=caus_o), [cb], [causb_b])
            En_b = Buf("En")
            if has_dsa:
                En = sb(pes, nc, "En", [128, NHA, 3, 128], BF16)
                T.dma(En[:].rearrange("p a b c -> p (a b c)"), En_d, writes=[En_b])

            kT_s = [sb(pes, nc, "kT%d" % i, [128, S], BF16) for i in range(2)]
            kT_sb = [Buf("kT%d" % i) for i in range(2)]
            qT_s = [sb(pes, nc, "qT%d" % i, [128, SO], BF16) for i in range(2)]
            qT_sb = [Buf("qT%d" % i) for i in range(2)]
            V_s = [sb(pes, nc, "V%d" % i, [128, NT, 132], BF16) for i in range(2)]
            V_sb = [Buf("V%d" % i) for i in range(2)]
            for i_ in range(2):
                T.op("dve", lambda h, i_=i_: h.memset(V_s[i_][:, :, 128:132], 1.0), [], [V_sb[i_]])
            NG = 5
            eb = [sb(pes, nc, "eb%d" % i, [128, 512], BF16) for i in range(NG)]
            eb_b = [Buf("eb%d" % i) for i in range(NG)]
            pb = [sb(pes, nc, "pb%d" % i, [128, 512], BF16) for i in range(NG)]
            pb_b = [Buf("pb%d" % i) for i in range(NG)]
            mk = [sb(pes, nc, "mk%d" % i, [128, NT, 128], BF16) for i in range(4 if has_dsa else 0)]
            mk_b = [Buf("mk%d" % i) for i in range(4)]
            Ff = [sb(pes, nc, "Ff%d" % i, [128, NT], F32) for i in range(4)]
            Ff_b = [Buf("Ff%d" % i) for i in range(4)]
            ob = [sb(pes, nc, "ob%d" % i, [128, 128], BF16) for i in range(2)]
            ob_b = [Buf("ob%d" % i) for i in range(2)]
            rc = [sb(pes, nc, "rc%d" % i, [128, 2], F32) for i in range(2)]
            rc_b = [Buf("rc%d" % i) for i in range(2)]
            otb = [sb(pes, nc, "otb%d" % i, [128, 128], BF16) for i in range(2)]
            otb_b = [Buf("otb%d" % i) for i in range(2)]
            gcnt = [0]
            ocnt = [0]
            scale = float(128 ** -0.5)
            LA = 3

            def run_head(kind, hh, hd_row, slot, fbias=None, fb_b=None):
                kT, kT_b, qT, qT_b, V, V_b = kT_s[slot], kT_sb[slot], qT_s[slot], qT_sb[slot], V_s[slot], V_sb[slot]
                tasks = []
                for g in range(NTO):
                    kbl = list(range(0, g + 1)) + list(range(NTO, NTO + g + 1))
                    nkb = len(kbl)
                    oi = ocnt[0] % 2
                    ocnt[0] += 1
                    grp = [(k0, min(nkb, k0 + 4)) for k0 in range(0, nkb, 4)]
                    for ii, (k0, k1) in enumerate(grp):
                        gi = gcnt[0] % NG
                        psid = gcnt[0] % 4
                        gcnt[0] += 1
                        tasks.append(dict(g=g, nkb=nkb, k0=k0, k1=k1, first=(ii == 0), last=(ii == len(grp) - 1), oi=oi, po=6 + oi,
                                          gi=gi, psid=psid, qi=g % 4, kbl=kbl))

                def qk(t):
                    g, k0, k1, psid, nkb, qi = t["g"], t["k0"], t["k1"], t["psid"], t["nkb"], t["qi"]
                    kbl = t["kbl"]
                    if t["first"]:
                        if kind == "dsa":
                            for sbase in (0, NTO):
                                for a0 in range(0, g + 1, 8):
                                    a1 = min(g + 1, a0 + 8)
                                    T.dma(mk[qi][:, sbase + a0:sbase + a1, :],
                                          maskT_d[g * NT + sbase + a0:g * NT + sbase + a1].rearrange("k p t -> p k t"), writes=[mk_b[qi]])
                        else:
                            T.op("act", lambda h, g=g, qi=qi: h.activation(out=Ff[qi][:, :], in_=fbias[:, g, :], func=AF.Exp),
                                 [fb_b], [Ff_b[qi]])
                    for ki_ in range(k0, k1):
                        kb = kbl[ki_]
                        T.op("pe", lambda h, kb=kb, ki_=ki_, k0=k0, psid=psid, g=g: h.matmul(
                            out=ps(psid)[:, (ki_ - k0) * 128:(ki_ - k0 + 1) * 128], lhsT=kT[:, kb * 128:(kb + 1) * 128],
                            rhs=qT[:, g * 128:(g + 1) * 128], start=True, stop=True), [kT_b, qT_b], [PB[psid]])

                def soft(t):
                    g, k0, k1, psid, gi, qi, kbl = t["g"], t["k0"], t["k1"], t["psid"], t["gi"], t["qi"], t["kbl"]
                    n = (k1 - k0) * 128
                    T.op("act", lambda h, gi=gi, psid=psid, n=n: h.activation(out=eb[gi][:, :n], in_=ps(psid)[:, :n], func=AF.Exp, scale=scale),
                         [PB[psid]], [eb_b[gi]])
                    runs = []
                    for ki_ in range(k0, k1):
                        if runs and kbl[ki_] == runs[-1][1] + runs[-1][2]:
                            runs[-1][2] += 1
                        else:
                            runs.append([ki_, kbl[ki_], 1])
                    for (ki0, kb0, cnt_) in runs:
                        o0 = (ki0 - k0) * 128
                        src = mk[qi][:, kb0:kb0 + cnt_, :] if kind == "dsa" else Ff[qi][:, kb0:kb0 + cnt_].unsqueeze(2).to_broadcast([128, cnt_, 128])
                        srcb = mk_b[qi] if kind == "dsa" else Ff_b[qi]
                        T.op("dve", lambda h, gi=gi, o0=o0, cnt_=cnt_, src=src: h.tensor_tensor(
                            out=pb[gi][:, o0:o0 + cnt_ * 128].rearrange("p (a b) -> p a b", a=cnt_),
                            in0=eb[gi][:, o0:o0 + cnt_ * 128].rearrange("p (a b) -> p a b", a=cnt_), in1=src, op=ALU.mult),
                            [eb_b[gi], srcb], [pb_b[gi]])
                    for ki_ in range(k0, k1):
                        kb = kbl[ki_]
                        o0 = (ki_ - k0) * 128
                        if kind == "dsa":
                            nj = 0 if kb == g else (1 if kb == NTO + g else (2 if kb == NTO + g - 1 else None))
                            if nj is not None:
                                T.op("dve", lambda h, gi=gi, o0=o0, nj=nj: h.tensor_tensor(
                                    out=pb[gi][:, o0:o0 + 128], in0=pb[gi][:, o0:o0 + 128], in1=En[:, hh, nj, :], op=ALU.mult),
                                    [pb_b[gi], En_b], [pb_b[gi]])
                        else:
                            nj = 0 if kb == g else (1 if kb == NTO + g else None)
                            if nj is not None:
                                T.op("dve", lambda h, gi=gi, o0=o0, nj=nj: h.tensor_tensor(
                                    out=pb[gi][:, o0:o0 + 128], in0=pb[gi][:, o0:o0 + 128], in1=causb[:, nj, :], op=ALU.mult),
                                    [pb_b[gi], causb_b], [pb_b[gi]])

                def pv(t):
                    g, k0, k1, gi, po, nkb, oi, kbl = t["g"], t["k0"], t["k1"], t["gi"], t["po"], t["nkb"], t["oi"], t["kbl"]
                    for ki_ in range(k0, k1):
                        kb = kbl[ki_]
                        T.op("pe", lambda h, gi=gi, kb=kb, ki_=ki_, k0=k0, po=po, nkb=nkb: h.matmul(
                            out=ps(po)[:, :130], lhsT=pb[gi][:, (ki_ - k0) * 128:(ki_ - k0 + 1) * 128], rhs=V[:, kb, 0:130],
                            start=(ki_ == 0), stop=(ki_ == nkb - 1)), [pb_b[gi], V_b], [PB[po]])
                    if t["last"]:
                        T.op("dve", lambda h, oi=oi, po=po: h.reciprocal(out=rc[oi][:, 0:1], in_=ps(po)[:, 128:129]), [PB[po]], [rc_b[oi]])
                        T.op("dve", lambda h, oi=oi, po=po: h.tensor_scalar(out=ob[oi][:], in0=ps(po)[:, 0:128], scalar1=rc[oi][:, 0:1], scalar2=None,
                                                                          op0=ALU.mult), [PB[po], rc_b[oi]], [ob_b[oi]])
                        pt = 4 + oi
                        T.op("pe", lambda h, oi=oi, pt=pt: h.transpose(out=ps_bf(pt)[:, 0:128], in_=ob[oi][:], identity=identb[:]),
                             [ob_b[oi], identb_b], [PB[pt]])
                        T.op("act", lambda h, oi=oi, pt=pt: h.activation(out=otb[oi][:], in_=ps_bf(pt)[:, 0:128], func=AF.Copy), [PB[pt]], [otb_b[oi]])
                        T.dma(oT[hd_row:hd_row + 128, g * 128:(g + 1) * 128], otb[oi][:], reads=[otb_b[oi]])

                for idx in range(len(tasks) + LA):
                    if idx < len(tasks):
                        qk(tasks[idx])
                    j = idx - LA
                    if j >= 0:
                        soft(tasks[j])
                        pv(tasks[j])

            lf = sb(pes, nc, "lf", [128, NT, NHB], F32)
            lf_b = Buf("lf")
            for a0 in range(0, NT, 8):
                a1 = min(NT, a0 + 8)
                T.dma(lf[:, a0:a1, :], lf_d.rearrange("(kb p) h -> p kb h", p=128)[:, a0:a1, :], writes=[lf_b])
            cum = sb(pes, nc, "cum", [128, NT, NHB], F32)
            cum_b = Buf("cum")
            cref = sb(pes, nc, "cref", [128, NT, NHB], F32)
            cref_b = Buf("cref")
            NW = NT * NHB
            for c0 in range(0, NW, 512):
                cw = min(512, NW - c0)
                lfv = lf[:].rearrange("p a b -> p (a b)")
                T.op("pe", lambda h, c0=c0, cw=cw, lfv=lfv: h.matmul(out=ps(0)[:, :cw], lhsT=utri, rhs=lfv[:, c0:c0 + cw], start=True, stop=True), [cb, lf_b], [PB[0]])
                T.op("dve", lambda h, c0=c0, cw=cw: h.tensor_copy(out=cum[:].rearrange("p a b -> p (a b)")[:, c0:c0 + cw], in_=ps(0)[:, :cw]), [PB[0]], [cum_b])
            totT = sb(pes, nc, "totT", [128, NHB], F32)
            totT_b = Buf("totT")
            onesc = sb(pes, nc, "onesc", [128, 128], F32)
            onesc_b = Buf("onesc")
            T.op("pool", lambda h: h.memset(onesc[:], 1.0), [], [onesc_b])
            for hh in range(NHB):
                T.op("pe", lambda h, hh=hh: h.matmul(out=ps(3)[:NT, hh:hh + 1], lhsT=lf[:, :, hh], rhs=onesc[:, 0:1], start=True, stop=True),
                     [lf_b, onesc_b], [PB[3]])
            T.op("dve", lambda h: h.tensor_copy(out=totT[:NT, :], in_=ps(3)[:NT, :NHB]), [PB[3]], [totT_b])
            Rm = sb(pes, nc, "Rm", [128, NT, NHB], F32)
            Rm_b = Buf("Rm")
            T.op("dve", lambda h: h.tensor_tensor(out=Rm[:NT, :, :], in0=cmat[:NT, :NT].unsqueeze(2).to_broadcast([NT, NT, NHB]),
                                                  in1=totT[:NT, :].unsqueeze(1).to_broadcast([NT, NT, NHB]), op=ALU.mult), [cb, totT_b], [Rm_b])
            for c0 in range(0, NW, 512):
                cw = min(512, NW - c0)
                rv = Rm[:].rearrange("p a b -> p (a b)")
                T.op("pe", lambda h, c0=c0, cw=cw, rv=rv: h.matmul(out=ps(1)[:, :cw], lhsT=onesc[:NT, :], rhs=rv[:NT, c0:c0 + cw], start=True, stop=True),
                     [onesc_b, Rm_b], [PB[1]])
                T.op("dve", lambda h, c0=c0, cw=cw: h.tensor_tensor(out=cum[:].rearrange("p a b -> p (a b)")[:, c0:c0 + cw],
                                                                   in0=cum[:].rearrange("p a b -> p (a b)")[:, c0:c0 + cw], in1=ps(1)[:, :cw], op=ALU.add),
                     [PB[1], cum_b], [cum_b])
            for c0 in range(0, NW, 512):
                cw = min(512, NW - c0)
                cv = cum[:].rearrange("p a b -> p (a b)")
                T.op("pe", lambda h, c0=c0, cw=cw, cv=cv: h.matmul(out=ps(2)[:, :cw], lhsT=sel0, rhs=cv[:, c0:c0 + cw], start=True, stop=True), [cb, cum_b], [PB[2]])
                T.op("dve", lambda h, c0=c0, cw=cw: h.tensor_copy(out=cref[:].rearrange("p a b -> p (a b)")[:, c0:c0 + cw], in_=ps(2)[:, :cw]), [PB[2]], [cref_b])
            ck_b, ukf_b, ukT_b, uvf_b, uvb_b = Buf("ck"), Buf("ukf"), Buf("ukT"), Buf("uvf"), Buf("uvb")
            if has_dsa:
                ck = sb(pes, nc, "ck", [128, CC, S], BF16)
                T.dma(ck[:], ckvT.rearrange("(cc p) s -> p cc s", p=128), writes=[ck_b])
                ukf = sb(pes, nc, "ukf", [128, CL], F32)
                ukT = sb(pes, nc, "ukT", [128, CC, 128], BF16)
                uvf = sb(pes, nc, "uvf", [128, CC, 128], F32)
                uvb = sb(pes, nc, "uvb", [128, CC, 128], BF16)
            def prep_dsa(hh, slot):
                kT, kT_b, qT, qT_b, V, V_b = kT_s[slot], kT_sb[slot], qT_s[slot], qT_sb[slot], V_s[slot], V_sb[slot]
                T.dma(ukf[:], w_uk[hh], writes=[ukf_b])
                for cc in range(CC):
                    T.op("pe", lambda h, cc=cc: h.transpose(out=ps(5)[:, cc * 128:(cc + 1) * 128], in_=ukf[:, cc * 128:(cc + 1) * 128], identity=ident_f),
                         [ukf_b, cb], [PB[5]])
                T.op("act", lambda h: h.activation(out=ukT[:], in_=ps(5)[:, :CC * 128].rearrange("p (a b) -> p a b", a=CC), func=AF.Copy), [PB[5]], [ukT_b])
                T.dma(uvf[:], w_uv[hh].rearrange("(cc p) d -> p cc d", p=128), writes=[uvf_b])
                T.op("dve", lambda h: h.tensor_copy(out=uvb[:], in_=uvf[:]), [uvf_b], [uvb_b])
                T.dma(qT[:, :SO], qaT[hh * 128:(hh + 1) * 128, :], writes=[qT_b])
                for c0 in range(0, S, 512):
                    pi = (c0 // 512) % 4
                    cw = min(512, S - c0)
                    for cc in range(CC):
                        T.op("pe", lambda h, pi=pi, cc=cc, c0=c0, cw=cw: h.matmul(out=ps(pi)[:, :cw], lhsT=ukT[:, cc, :], rhs=ck[:, cc, c0:c0 + cw],
                                                                                start=(cc == 0), stop=(cc == CC - 1)), [ukT_b, ck_b], [PB[pi]])
                    T.op("act", lambda h, pi=pi, c0=c0, cw=cw: h.activation(out=kT[:, c0:c0 + cw], in_=ps(pi)[:, :cw], func=AF.Copy), [PB[pi]], [kT_b])
                for kb0 in range(0, NT, 4):
                    kb1 = min(NT, kb0 + 4)
                    pi = 4 + ((kb0 // 4) % 2)
                    for kb in range(kb0, kb1):
                        for cc in range(CC):
                            T.op("pe", lambda h, pi=pi, kb=kb, kb0=kb0, cc=cc: h.matmul(
                                out=ps(pi)[:, (kb - kb0) * 128:(kb - kb0 + 1) * 128], lhsT=ck[:, cc, kb * 128:(kb + 1) * 128], rhs=uvb[:, cc, :],
                                start=(cc == 0), stop=(cc == CC - 1)), [ck_b, uvb_b], [PB[pi]])
                    T.op("dve", lambda h, pi=pi, kb0=kb0, kb1=kb1: h.tensor_copy(
                        out=V[:, kb0:kb1, 0:128], in_=ps(pi)[:, :(kb1 - kb0) * 128].rearrange("p (a b) -> p a b", a=kb1 - kb0)), [PB[pi]], [V_b])

            fbias_s = [sb(pes, nc, "fbias%d" % i, [128, NTO, NT], F32) for i in range(2)]
            fb_sb = [Buf("fbias%d" % i) for i in range(2)]

            def prep_fox(hh, slot):
                kT, kT_b, qT, qT_b, V, V_b = kT_s[slot], kT_sb[slot], qT_s[slot], qT_sb[slot], V_s[slot], V_sb[slot]
                fbias, fb_b = fbias_s[slot], fb_sb[slot]
                for g in range(NTO):
                    T.op("dve", lambda h, g=g, hh=hh: h.tensor_scalar(out=fbias[:, g, :], in0=cum[:, :, hh], scalar1=-1.0, scalar2=cref[:, g, hh:hh + 1],
                                                                     op0=ALU.mult, op1=ALU.add), [cum_b, cref_b], [fb_b])
                T.op("dve", lambda h: h.tensor_scalar(out=fbias[:], in0=fbias[:], scalar1=80.0, scalar2=None, op0=ALU.min), [fb_b], [fb_b])
                T.dma(kT[:], kbT[hh * 128:(hh + 1) * 128, :], writes=[kT_b])
                T.dma(qT[:, :SO], qbT[hh * 128:(hh + 1) * 128, :], writes=[qT_b])
                for a0 in range(0, NT, 8):
                    a1 = min(NT, a0 + 8)
                    T.dma(V[:, a0:a1, 0:128], vb_d.rearrange("(kb p) w -> p kb w", p=128)[:, a0:a1, hh * 128:(hh + 1) * 128], writes=[V_b])

            heads = [hd for hd in ([("dsa", hh) for hh in range(NHA)] + [("fox", hh) for hh in range(NHB)]) if hd[0] in kinds]

            def prep(i):
                kind, hh = heads[i]
                if kind == "dsa":
                    prep_dsa(hh, i % 2)
                else:
                    prep_fox(hh, i % 2)

            def run_i(i):
                kind, hh = heads[i]
                if i + 1 < len(heads):
                    prep(i + 1)
                if kind == "dsa":
                    run_head("dsa", hh, hh * 128, i % 2)
                else:
                    run_head("fox", hh, WA + hh * 128, i % 2, fbias_s[i % 2], fb_sb[i % 2])

            nh = len(heads)
            done_h = 0
            if nh:
                prep(0)
            if idx_qb is not None:
                for j in range(NTO):
                    idx_qb(j)
                    while done_h < (j + 1) * nh // NTO:
                        run_i(done_h)
                        done_h += 1
            while done_h < nh:
                run_i(done_h)
                done_h += 1
            end_phase()

    def phase_wout():
        if not bg.done():
            with ExitStack() as pes:
                bg.engs = ["dve", "act"]
                bg.lq, bg.sq = None, None
                bg.attach(pes, "bgw")
                bg.run_all()
                bg.flush_inflight()
                end_phase()
        with ExitStack() as pes:
            oTs = sb(pes, nc, "oTs", [128, KC, TT], BF16)
            oTs_b = Buf("oTs")
            NWB = 2
            wb_t = [sb(pes, nc, "wb%d" % i, [128, KC * DNW], BF16) for i in range(NWB)]
            wb_b = [Buf("wb%d" % i) for i in range(NWB)]
            xp = [sb(pes, nc, "xp%d" % i, [128, DNW], F32) for i in range(2)]
            xp_b = [Buf("xp%d" % i) for i in range(2)]
            rp = [sb(pes, nc, "rp%d" % i, [128, DNW], F32) for i in range(2)]
            rp_b = [Buf("rp%d" % i) for i in range(2)]
            wcnt = [0]
            pcnt = [0]
            for t0 in range(0, SO, TT):
                for a0 in range(0, KC, 8):
                    a1 = min(KC, a0 + 8)
                    T.dma(oTs[:, a0:a1, :], oT.rearrange("(kc p) s -> p kc s", p=128)[:, a0:a1, t0:t0 + TT], writes=[oTs_b])
                for n in range(NDN):
                    wi = wcnt[0] % NWB
                    wcnt[0] += 1
                    T.dma(wb_t[wi][:], wout_s[n], writes=[wb_b[wi]])
                    wv = wb_t[wi][:].rearrange("p (kc f) -> p kc f", kc=KC)
                    for tb in range(TBT):
                        pi = pcnt[0] % 4
                        k = pcnt[0] % 2
                        pcnt[0] += 1
                        rows = slice(t0 + tb * 128, t0 + (tb + 1) * 128)
                        for kc in range(KC):
                            T.op("pe", lambda h, pi=pi, kc=kc, tb=tb, wv=wv: h.matmul(
                                out=ps(pi)[:, :DNW], lhsT=oTs[:, kc, tb * 128:(tb + 1) * 128], rhs=wv[:, kc, :],
                                start=(kc == 0), stop=(kc == KC - 1)), [oTs_b, wb_b[wi]], [PB[pi]])
                        T.dma(xp[k][:], h1[rows, n * DNW:(n + 1) * DNW], writes=[xp_b[k]])
                        T.op("dve", lambda h, k=k, pi=pi: h.scalar_tensor_tensor(
                            out=rp[k][:], in0=xp[k][:], scalar=float(ALPHA), in1=ps(pi)[:, :DNW], op0=ALU.mult, op1=ALU.add),
                            [xp_b[k], PB[pi]], [rp_b[k]])
                        T.dma(r_pre[rows, n * DNW:(n + 1) * DNW], rp[k][:], reads=[rp_b[k]])
            end_phase()

    phases = [("precast", phase_precast), ("ffn1", lambda: phase_ffn(0, x, S)), ("ln1", lambda: phase_ln(0, h1, S)),
              ("win", phase_win), ("idxfox", lambda: phase_attn(("fox",), True)), ("attn", lambda: phase_attn(("dsa",), False)), ("wout", phase_wout),
              ("ln2", lambda: phase_ln(1, h2, SO)), ("ffn2", lambda: phase_ffn(1, h2, SO)), ("ln3", lambda: phase_ln(2, out, SO))]
    dbg_out = {"ln1": h1, "ln2": h2}
    for nm, fn in phases:
        fn()
        if stop_after == nm:
            if nm in dbg_out:
                with ExitStack() as pes:
                    t_ = sb(pes, nc, "dbg", [128, D], F32)
                    tb_ = Buf("dbg")
                    for tb in range(NTO):
                        T.dma(t_[:], dbg_out[nm][tb * 128:(tb + 1) * 128, :], writes=[tb_])
                        T.dma(out[tb * 128:(tb + 1) * 128, :], t_[:], reads=[tb_])
                    end_phase()
            if nm == "attn":
                with ExitStack() as pes:
                    t1 = sb(pes, nc, "dbg1", [128, SO], BF16)
                    t2 = sb(pes, nc, "dbg2", [128, SO], F32)
                    b1, b2 = Buf("dbg1"), Buf("dbg2")
                    ov = out.rearrange("s d -> (s d)").rearrange("(a b) -> a b", b=SO)
                    for kc in range(KC):
                        T.dma(t1[:], oT[kc * 128:(kc + 1) * 128, :], writes=[b1])
                        T.op("dve", lambda h: h.tensor_copy(out=t2[:], in_=t1[:]), [b1], [b2])
                        T.dma(ov[kc * 128:(kc + 1) * 128, :], t2[:], reads=[b2])
                    end_phase()
            break
    es.close()
    return nc


def _bucket(rel):
    nb = NBK // 2
    me = nb // 2
    ret = np.where(rel > 0, nb, 0)
    n = np.abs(rel)
    nf = np.maximum(n, 1).astype(np.float32)
    large = me + (np.log(nf / np.float32(me)) / np.float32(math.log(128 / me)) * np.float32(nb - me)).astype(np.int32)
    large = np.minimum(large, nb - 1)
    return ret + np.where(n < me, n, large)


def host_consts(r, NT):
    NTO = NT // 2
    c = np.zeros((128, 13, 128), np.float32)
    p = np.arange(128)
    c[:, 0, :] = np.eye(128, dtype=np.float32)
    t = p[:, None]
    s = p[None, :]
    c[:, 1, :] = np.where((s // 64) <= (t // 64), 0.0, NEGBIG)
    c[:, 2, :] = 0.0 if r == 1 else NEGBIG
    sT = p[:, None]
    tT = p[None, :]
    c[:, 3, :] = ((sT // 64) <= (tT // 64)).astype(np.float32)
    c[:, 4, :] = _bucket(sT - tT).astype(np.float32)
    d_oj = -1 if r == 1 else 1
    d_ojm1 = -3 if r == 1 else -1
    c[:, 5, :] = _bucket(sT + d_oj * 128 - tT).astype(np.float32)
    c[:, 6, :] = _bucket(sT + d_ojm1 * 128 - tT).astype(np.float32)
    c[:, 7, :] = (sT <= tT).astype(np.float32)
    c[:, 8, :] = 1.0 if r == 1 else 0.0
    c[:, 9, :] = (p[:, None] <= p[None, :]).astype(np.float32)
    c[:, 10, :] = (p[:, None] == 0).astype(np.float32) * np.ones((1, 128), np.float32)
    glob = np.array([2 * lb + r if lb < NTO else 2 * (lb - NTO) + (1 - r) for lb in range(NT)])
    cm = (glob[:, None] < glob[None, :]).astype(np.float32)
    c[:NT, 11, :NT] = cm
    return c


def block_order(r, NT):
    own = [g for g in range(NT) if g % 2 == r]
    oth = [g for g in range(NT) if g % 2 != r]
    return own, oth


def make_in_map(inputs, b, r, cfg):
    c = derive(cfg)
    NT, D = c["NT"], c["D"]
    f = lambda a: np.ascontiguousarray(np.asarray(a, dtype=np.float32))
    own, oth = block_order(r, NT)
    xb = np.asarray(inputs["x"][b], dtype=np.float32).reshape(NT, 128, D)
    xp = np.ascontiguousarray(xb[own + oth].reshape(NT * 128, D))
    m = {
        "x": xp,
        "ffn1_w_gate": f(inputs["ffn1_w_gate"][0]), "ffn1_w_up": f(inputs["ffn1_w_up"][0]), "ffn1_w_down": f(inputs["ffn1_w_down"][0]),
        "ffn2_w_gate": f(inputs["ffn2_w_gate"][0]), "ffn2_w_up": f(inputs["ffn2_w_up"][0]), "ffn2_w_down": f(inputs["ffn2_w_down"][0]),
        "ln1_g": f(inputs["ln1_g"]), "ln1_b": f(inputs["ln1_b"]), "ln2_g": f(inputs["ln2_g"]), "ln2_b": f(inputs["ln2_b"]),
        "ln3_g": f(inputs["ln3_g"]), "ln3_b": f(inputs["ln3_b"]),
        "w_in": f(inputs["w_in"][0]), "b_f": f(inputs["b_f"]), "kv_norm_g": f(inputs["kv_norm_g"]),
        "idx_k_g": f(inputs["idx_k_g"]), "idx_k_b": f(inputs["idx_k_b"]),
        "w_uk": f(inputs["w_uk"][0]), "w_uv": f(inputs["w_uv"][0]),
        "rel_bias": f(np.asarray(inputs["rel_bias"]).reshape(1, -1)),
        "w_out": f(inputs["w_out"][0]), "consts": host_consts(r, NT),
    }
    return m


def run(inputs, cfg, stop_after=None):
    c = derive(cfg)
    NT, D, S = c["NT"], c["D"], c["S"]
    nb = np.asarray(inputs["x"]).shape[0]
    n_cores = 2 * nb
    nc = build_program(cfg, stop_after=stop_after)
    in_maps = [make_in_map(inputs, cidx // 2, cidx % 2, cfg) for cidx in range(n_cores)]
    res = run_bass_kernel_spmd(nc, in_maps, core_ids=list(range(n_cores)))
    if stop_after is not None:
        return [np.asarray(res.results[cidx]["out"]) for cidx in range(n_cores)]
    out = np.zeros((nb, NT, 128, D), np.float32)
    for cidx in range(n_cores):
        own, _ = block_order(cidx % 2, NT)
        out[cidx // 2, own] = np.asarray(res.results[cidx]["out"], dtype=np.float32).reshape(NT // 2, 128, D)
    return out.reshape(nb, S, D)


def kernel(**inputs):
    return run(inputs, FULL_CFG)
```
